# Optimizing a Trainium2 kernel written in Bass

```python
import math
import jax, jax.numpy as jnp
from jax import lax
import numpy as np


D_MODEL = 2048
BATCH = 2
SEQ = 8192
DEPTH = 4

F32 = jnp.float32
EPS = 1e-6

HG_HEADS = 6
HG_DK = 128
HG_DV = 128
HG_WIDTH = HG_HEADS * HG_DV
HG_CHUNK = 64

RET_HEADS = 6
RET_DK = 64
RET_DV = 128
RET_WIDTH = RET_HEADS * RET_DV
RET_CHUNK = 128
ROPE_BASE = 10000.0

DIL_SLOTS = 4
DIL_HD = 128
DIL_GROUPS = ((128, 1), (512, 4), (2048, 16))
DIL_WIDTH = DIL_SLOTS * DIL_HD
DIL_HEADS = DIL_SLOTS * len(DIL_GROUPS)

MIX_WIDTH = HG_WIDTH + RET_WIDTH + DIL_WIDTH
D_FF = 4 * D_MODEL
REL_BUCKETS = 32
REL_MAX_DIST = 1024

IN_SPLITS = (HG_HEADS * HG_DK, HG_WIDTH, HG_HEADS * HG_DK, HG_HEADS * HG_DK, HG_WIDTH,
             RET_HEADS * RET_DK, RET_HEADS * RET_DK, RET_WIDTH, RET_WIDTH) + (DIL_WIDTH,) * (3 * len(DIL_GROUPS))
IN_WIDTH = sum(IN_SPLITS)

kernel_name = 'hybrid_hgrn2_retnet_dilated_encoder'


def rms_norm(x, g):
    xf = x.astype(F32)
    y = xf * lax.rsqrt(jnp.mean(xf * xf, axis=-1, keepdims=True) + EPS)
    return (y * g.astype(F32)).astype(x.dtype)


def head_rms(t, gain):
    return t * lax.rsqrt(jnp.mean(t * t, axis=-1, keepdims=True) + EPS) * gain.astype(F32)


def to_heads(t, n_heads):
    B, S, W = t.shape
    return t.astype(F32).reshape(B, S, n_heads, W // n_heads).transpose(0, 2, 1, 3)


def rope(t):
    S, d = t.shape[1], t.shape[-1]
    half = d // 2
    inv = ROPE_BASE ** (-jnp.arange(half, dtype=F32) / half)
    ang = jnp.arange(S, dtype=F32)[:, None] * inv[None, :]
    cos = jnp.cos(ang)[None, :, None, :]
    sin = jnp.sin(ang)[None, :, None, :]
    t1, t2 = t[..., :half], t[..., half:]
    return jnp.concatenate([t1 * cos - t2 * sin, t1 * sin + t2 * cos], axis=-1)


def hgrn2_chunk_scan(q, k, v, log_f):
    B, H, S, DK = q.shape
    DV = v.shape[-1]
    C = HG_CHUNK
    nc = S // C

    def chunks(t):
        return t.reshape(B, H, nc, C, t.shape[-1]).transpose(2, 0, 1, 3, 4)

    mask = jnp.tril(jnp.ones((C, C), dtype=bool))[:, :, None]

    def step(state, inp):
        qc, kc, vc, lc = inp
        b = jnp.cumsum(lc, axis=2)
        diff = b[:, :, :, None, :] - b[:, :, None, :, :]
        decay = jnp.exp(jnp.where(mask, diff, -jnp.inf))
        attn = jnp.einsum('bhik,bhjk,bhijk->bhij', qc, kc, decay)
        out = (jnp.einsum('bhij,bhjv->bhiv', attn, vc)
               + jnp.einsum('bhik,bhkv->bhiv', qc * jnp.exp(b), state))
        b_last = b[:, :, -1:, :]
        state = (jnp.exp(b_last)[:, :, 0, :, None] * state
                 + jnp.einsum('bhjk,bhjv->bhkv', kc * jnp.exp(b_last - b), vc))
        return state, out

    state0 = jnp.zeros((B, H, DK, DV), F32)
    _, out = lax.scan(step, state0, (chunks(q), chunks(k), chunks(v), chunks(log_f)))
    return out.transpose(1, 2, 0, 3, 4).reshape(B, H, S, DV)


def hgrn2_mixer(q, i_in, z_fwd, z_bwd, gate, lb_fwd, lb_bwd, norm_g):
    B, S, _ = q.shape
    qh = to_heads(q, HG_HEADS)
    vh = to_heads(i_in, HG_HEADS)

    def one_direction(z, lb, reverse):
        lbh = lb.astype(F32).reshape(HG_HEADS, 1, HG_DK)
        f = lbh + (1.0 - lbh) * jax.nn.sigmoid(to_heads(z, HG_HEADS))
        k = 1.0 - f
        lf = jnp.log(f)
        if reverse:
            fl = lambda t: jnp.flip(t, axis=2)
            return fl(hgrn2_chunk_scan(fl(qh), fl(k), fl(vh), fl(lf)))
        return hgrn2_chunk_scan(qh, k, vh, lf)

    o = one_direction(z_fwd, lb_fwd, False) + one_direction(z_bwd, lb_bwd, True)
    o = o.transpose(0, 2, 1, 3)
    o = o * lax.rsqrt(jnp.mean(o * o, axis=-1, keepdims=True) + EPS)
    o = o.reshape(B, S, HG_WIDTH) * norm_g.astype(F32)
    return o * jax.nn.silu(gate.astype(F32))


def retention_chunk(q, k, v, log_gamma):
    B, H, S, DK = q.shape
    DV = v.shape[-1]
    C = RET_CHUNK
    nc = S // C
    qc = q.reshape(B, H, nc, C, DK)
    kc = k.reshape(B, H, nc, C, DK)
    vc = v.reshape(B, H, nc, C, DV)
    idx = jnp.arange(C, dtype=F32)
    lg = log_gamma[:, None]
    rel = idx[:, None] - idx[None, :]
    decay = jnp.where(rel >= 0, jnp.exp(lg[:, :, None] * jnp.maximum(rel, 0.0)), 0.0)
    scores = jnp.einsum('bhnid,bhnjd->bhnij', qc, kc) * decay[None, :, None]
    intra = jnp.einsum('bhnij,bhnje->bhnie', scores, vc)
    zeta = jnp.exp(lg * (C - 1 - idx))
    xi = jnp.exp(lg * (idx + 1))
    kv = jnp.einsum('bhnjd,hj,bhnje->nbhde', kc, zeta, vc)
    chunk_decay = jnp.exp(log_gamma * C)[None, :, None, None]

    def step(state, kv_n):
        return chunk_decay * state + kv_n, state

    _, prev = lax.scan(step, jnp.zeros((B, H, DK, DV), F32), kv)
    cross = jnp.einsum('bhnid,hi,nbhde->bhnie', qc, xi, prev)
    return (intra + cross).reshape(B, H, S, DV)


def retention_mixer(q, k, v, gate, norm_g):
    B, S, _ = q.shape
    qh = rope(q.astype(F32).reshape(B, S, RET_HEADS, RET_DK)).transpose(0, 2, 1, 3)
    kh = rope(k.astype(F32).reshape(B, S, RET_HEADS, RET_DK)).transpose(0, 2, 1, 3) * RET_DK ** -0.5
    vh = to_heads(v, RET_HEADS)
    hidx = jnp.arange(RET_HEADS, dtype=F32)
    log_g_fwd = jnp.log1p(-jnp.exp2(-5.0 - hidx))
    log_g_bwd = log_g_fwd[::-1]
    fl = lambda t: jnp.flip(t, axis=2)
    o = retention_chunk(qh, kh, vh, log_g_fwd) + fl(retention_chunk(fl(qh), fl(kh), fl(vh), log_g_bwd))
    o = o.transpose(0, 2, 1, 3)
    mu = jnp.mean(o, axis=-1, keepdims=True)
    var = jnp.mean(jnp.square(o - mu), axis=-1, keepdims=True)
    o = ((o - mu) * lax.rsqrt(var + EPS)).reshape(B, S, RET_WIDTH) * norm_g.astype(F32)
    return o * jax.nn.silu(gate.astype(F32))


def t5_bucket(rel):
    nb = REL_BUCKETS // 2
    max_exact = nb // 2
    sign_off = jnp.where(rel > 0, nb, 0)
    n = jnp.abs(rel)
    nf = jnp.maximum(n, 1).astype(F32)
    large = max_exact + (jnp.log(nf / max_exact) / math.log(REL_MAX_DIST / max_exact)
                         * (nb - max_exact)).astype(jnp.int32)
    large = jnp.minimum(large, nb - 1)
    return sign_off + jnp.where(n < max_exact, n, large)


def dilated_local_attention(q, k, v, bias_table, dil, half):
    B, H, S, D = q.shape
    L = S // dil
    nb = -(-L // half)
    Lp = nb * half

    def to_res(t):
        return t.reshape(B, H, L, dil, D).transpose(0, 1, 3, 2, 4)

    qr, kr, vr = to_res(q), to_res(k), to_res(v)
    qb = jnp.pad(qr, ((0, 0), (0, 0), (0, 0), (0, Lp - L), (0, 0))).reshape(B, H, dil, nb, half, D)

    def band(t):
        tp = jnp.pad(t, ((0, 0), (0, 0), (0, 0), (half, Lp - L + half), (0, 0)))
        tp = tp.reshape(B, H, dil, nb + 2, half, D)
        return jnp.concatenate([tp[:, :, :, :-2], tp[:, :, :, 1:-1], tp[:, :, :, 2:]], axis=4)

    kb, vb = band(kr), band(vr)
    ii = jnp.arange(half)[:, None]
    jj = jnp.arange(3 * half)[None, :]
    rel = jj - half - ii
    bias = bias_table.astype(F32)[t5_bucket(rel * dil)].transpose(2, 0, 1)
    key_idx = jnp.arange(nb)[:, None, None] * half + jj[None] - half
    valid = (jnp.abs(rel) <= half)[None] & (key_idx >= 0) & (key_idx < L)
    s = jnp.einsum('bhrnqd,bhrnkd->bhrnqk', qb, kb) + bias[None, :, None, None]
    s = jnp.where(valid, s, -jnp.inf)
    m = jnp.max(s, axis=-1, keepdims=True)
    p = jnp.exp(s - m)
    den = jnp.sum(p, axis=-1)
    o = jnp.einsum('bhrnqk,bhrnkd->bhrnqd', p, vb) / den[..., None]
    lse = m[..., 0] + jnp.log(den)
    o = o.reshape(B, H, dil, Lp, D)[:, :, :, :L].transpose(0, 1, 3, 2, 4).reshape(B, H, S, D)
    lse = lse.reshape(B, H, dil, Lp)[..., :L].transpose(0, 1, 3, 2).reshape(B, H, S)
    return o, lse


def dilated_mixer(parts, rel_bias, q_gain, k_gain):
    B, S, _ = parts[0].shape
    outs, lses = [], []
    for g, (window, dil) in enumerate(DIL_GROUPS):
        half = window // (2 * dil)
        q = head_rms(parts[3 * g].astype(F32).reshape(B, S, DIL_SLOTS, DIL_HD), q_gain) * DIL_HD ** -0.5
        k = head_rms(parts[3 * g + 1].astype(F32).reshape(B, S, DIL_SLOTS, DIL_HD), k_gain)
        v = parts[3 * g + 2].astype(F32).reshape(B, S, DIL_SLOTS, DIL_HD)
        tbl = rel_bias[:, g * DIL_SLOTS:(g + 1) * DIL_SLOTS]
        o, lse = dilated_local_attention(q.transpose(0, 2, 1, 3), k.transpose(0, 2, 1, 3),
                                         v.transpose(0, 2, 1, 3), tbl, dil, half)
        outs.append(o)
        lses.append(lse)
    w = jax.nn.softmax(jnp.stack(lses, axis=0), axis=0)
    o = jnp.sum(w[..., None] * jnp.stack(outs, axis=0), axis=0)
    return o.transpose(0, 2, 1, 3).reshape(B, S, DIL_WIDTH)


def setup_inputs(seed: int = 0) -> dict:
    key = jax.random.key(seed)
    ks = jax.random.split(key, 14)

    def nrm(k, shape, scale):
        return jax.random.normal(k, shape, F32) * scale

    return {
        'x': nrm(ks[0], (BATCH, SEQ, D_MODEL), 1.0),
        'w_in': nrm(ks[1], (DEPTH, D_MODEL, IN_WIDTH), D_MODEL ** -0.5),
        'w_out': nrm(ks[2], (DEPTH, MIX_WIDTH, D_MODEL), MIX_WIDTH ** -0.5),
        'w_up': nrm(ks[3], (DEPTH, D_MODEL, D_FF), D_MODEL ** -0.5),
        'w_down': nrm(ks[4], (DEPTH, D_FF, D_MODEL), D_FF ** -0.5),
        'norm_mix': 1.0 + nrm(ks[5], (DEPTH, D_MODEL), 0.02),
        'norm_mlp': 1.0 + nrm(ks[6], (DEPTH, D_MODEL), 0.02),
        'hg_lb_fwd': nrm(ks[7], (DEPTH, HG_HEADS * HG_DK), 0.5),
        'hg_lb_bwd': nrm(ks[8], (DEPTH, HG_HEADS * HG_DK), 0.5),
        'hg_norm': 1.0 + nrm(ks[9], (DEPTH, HG_WIDTH), 0.02),
        'ret_norm': 1.0 + nrm(ks[10], (DEPTH, RET_WIDTH), 0.02),
        'q_norm': 1.0 + nrm(ks[11], (DEPTH, DIL_HD), 0.02),
        'k_norm': 1.0 + nrm(ks[12], (DEPTH, DIL_HD), 0.02),
        'rel_bias': nrm(ks[13], (REL_BUCKETS, DIL_HEADS), 0.1),
    }


def reference(x, w_in, w_out, w_up, w_down, norm_mix, norm_mlp, hg_lb_fwd, hg_lb_bwd,
              hg_norm, ret_norm, q_norm, k_norm, rel_bias):
    lb_fwd_all = jnp.cumsum(jax.nn.softmax(hg_lb_fwd.astype(F32), axis=0), axis=0)
    lb_bwd_all = jnp.cumsum(jax.nn.softmax(hg_lb_bwd.astype(F32), axis=0), axis=0)
    offsets = np.cumsum(IN_SPLITS)[:-1].tolist()
    for l in range(DEPTH):
        h = rms_norm(x, norm_mix[l]) @ w_in[l]
        p = jnp.split(h, offsets, axis=-1)
        y_a = hgrn2_mixer(p[0], p[1], p[2], p[3], p[4],
                          lb_fwd_all[l] - lb_fwd_all[0], lb_bwd_all[l] - lb_bwd_all[0], hg_norm[l])
        y_b = retention_mixer(p[5], p[6], p[7], p[8], ret_norm[l])
        y_c = dilated_mixer(p[9:], rel_bias, q_norm[l], k_norm[l])
        y = jnp.concatenate([y_a, y_b, y_c], axis=-1).astype(x.dtype)
        x = x + y @ w_out[l]
        hm = rms_norm(x, norm_mlp[l])
        x = x + jnp.square(jax.nn.relu(hm @ w_up[l])) @ w_down[l]
    return x
```

```python
import contextlib
import math
import numpy as np
import concourse.bass as bass
import concourse.mybir as mybir
from concourse.bass_utils import run_bass_kernel_spmd

F32 = mybir.dt.float32
BF16 = mybir.dt.bfloat16
ALU = mybir.AluOpType
AF = mybir.ActivationFunctionType
AX = mybir.AxisListType

D = 2048
INW = 10752
DFF = 8192
EPS = 1e-6
NEG = -1.0e30
DBG = {}
HG_Q, HG_I, HG_ZF, HG_ZB, HG_G = 0, 768, 1536, 2304, 3072
RT_Q, RT_K, RT_V, RT_G = 3840, 4224, 4608, 5376
DL0 = 6144


class Sched:
    ENGS = ("pe", "dve", "act", "pool", "sp")
    NDMA = 16

    def __init__(self, nc, same_engine_sync=True):
        self.nc = nc
        self.ops = []
        self.last_w = {}
        self.readers = {}
        self.same = same_engine_sync
        self.bar = set()
        self.since = set()
        self.lastop = {}

    EXCL = ("PA", "PO", "PKV", "PT", "PS", "PB")

    def op(self, eng, fn, reads=(), writes=(), dma=False):
        ex = [k for k in reads if k.startswith(self.EXCL)]
        if ex:
            reads = [k for k in reads if k not in ex]
            writes = list(writes) + [k for k in ex if k not in writes]
        deps = set(self.bar)
        for k in reads:
            if k in self.last_w:
                deps.add(self.last_w[k])
        for k in writes:
            if k in self.last_w:
                deps.add(self.last_w[k])
            deps.update(self.readers.get(k, ()))
        i = len(self.ops)
        self.ops.append(dict(eng=eng, fn=fn, deps=deps, dma=dma))
        for k in reads:
            self.readers.setdefault(k, []).append(i)
        for k in writes:
            self.last_w[k] = i
            self.readers[k] = []
        if dma:
            self.since.add(i)
        else:
            self.lastop[eng] = i
        return i

    def barrier(self):
        self.bar = set(self.lastop.values()) | self.since
        self.since = set()
        self.last_w = {}
        self.readers = {}

    def pe(self, fn, r=(), w=()): return self.op("pe", fn, r, w)
    def dve(self, fn, r=(), w=()): return self.op("dve", fn, r, w)
    def act(self, fn, r=(), w=()): return self.op("act", fn, r, w)
    def pool(self, fn, r=(), w=()): return self.op("pool", fn, r, w)
    def dma(self, q, fn, r=(), w=()): return self.op(q, fn, r, w, dma=True)

    def emit(self, final_wait_ops=()):
        nc = self.nc
        ops = self.ops
        n = len(ops)
        needed = [False] * n
        for o in ops:
            for d in o["deps"]:
                od = ops[d]
                if (not od["dma"]) and (not o["dma"]) and od["eng"] == o["eng"] and (o["eng"] == "pe" or not self.same):
                    continue
                needed[d] = True
        for d in final_wait_ops:
            needed[d] = True
        with contextlib.ExitStack() as st:
            esem = {e: st.enter_context(nc.semaphore("s_" + e)) for e in self.ENGS}
            dsem = [st.enter_context(nc.semaphore("d_%d" % i)) for i in range(self.NDMA)]
            ecnt = {e: 0 for e in self.ENGS}
            dcnt = [0] * self.NDMA
            rr = {"pool": 0, "hw": 0}
            half = self.NDMA // 2
            tok = [None] * n
            prevdma = [None] * n
            for i, o in enumerate(ops):
                if o["dma"]:
                    if o["eng"] == "pool":
                        s = rr["pool"] % half
                        rr["pool"] += 1
                    else:
                        s = half + rr["hw"] % half
                        rr["hw"] += 1
                    if dcnt[s] > 0:
                        prevdma[i] = (s, 16 * dcnt[s], None)
                    dcnt[s] += 1
                    tok[i] = (s, 16 * dcnt[s], None)
                elif needed[i]:
                    ecnt[o["eng"]] += 1
                    tok[i] = (o["eng"], ecnt[o["eng"]], o["eng"])
            self.stats = dict(n=n, ecnt=dict(ecnt), dcnt=list(dcnt))
            block = st.enter_context(nc.Block())

            def semof(k):
                return dsem[k] if isinstance(k, int) else esem[k]

            def run(ename):
                def body(eng):
                    waited = {}
                    for i, o in enumerate(ops):
                        if o["eng"] != ename:
                            continue
                        ws = []
                        for d in o["deps"]:
                            t = tok[d]
                            if t is None:
                                continue
                            if t[2] == ename and (ename == "pe" or not self.same):
                                continue
                            ws.append(t)
                        if prevdma[i] is not None:
                            ws.append(prevdma[i])
                        ws.sort(key=lambda t: -t[1])
                        for (sk, val, _) in ws:
                            if waited.get(sk, 0) >= val:
                                continue
                            waited[sk] = val
                            eng.wait_ge(semof(sk), val)
                        ins = o["fn"](eng)
                        if o["dma"]:
                            ins.then_inc(semof(tok[i][0]), 16)
                        elif needed[i]:
                            ins.then_inc(semof(tok[i][0]), 1)
                    if ename == "sp":
                        for d in final_wait_ops:
                            sk, val, _ = tok[d]
                            eng.wait_ge(semof(sk), val)
                return body

            block.tensor(run("pe"))
            block.vector(run("dve"))
            block.scalar(run("act"))
            block.gpsimd(run("pool"))
            block.sync(run("sp"))


def host_consts(T):
    c = {}
    NT = T // 128
    half = 32
    inv = (10000.0 ** (-np.arange(half, dtype=np.float32) / half)).astype(np.float32)
    ang = np.arange(T, dtype=np.float32)[:, None] * inv[None, :]
    cos = np.cos(ang).astype(np.float32).reshape(NT, 128, half).transpose(1, 0, 2)
    sin = np.sin(ang).astype(np.float32).reshape(NT, 128, half).transpose(1, 0, 2)
    c["rope"] = np.ascontiguousarray(np.concatenate([cos, cos, -sin, sin], axis=2))
    j = np.arange(128)[:, None]
    i = np.arange(128)[None, :]
    same = (j // 32) == (i // 32)
    hgm = np.zeros((128, 6, 128), np.float32)
    hgm[:, 0] = same & (j <= i)
    hgm[:, 1] = same & (j > i)
    hgm[:, 2] = same & (j >= i)
    hgm[:, 3] = same & (j < i)
    hgm[:, 4] = same & (j <= i)
    hgm[:, 5] = same & (j >= i)
    c["hgm"] = hgm
    ind = np.zeros((128, 4), np.float32)
    ind[np.arange(128), np.arange(128) // 32] = 1.0
    c["ind"] = ind
    c["ident"] = np.eye(128, dtype=np.float32)
    hidx = np.arange(6, dtype=np.float64)
    lgf = np.log1p(-np.exp2(-5.0 - hidx))
    lgb = lgf[::-1]
    idx = np.arange(128, dtype=np.float64)
    rel = idx[None, :] - idx[:, None]
    rmask = np.zeros((128, 6, 128), np.float32)
    rpp = np.zeros((128, 6, 4), np.float32)
    rxi = np.zeros((64, 6, 2, 128), np.float32)
    rcd = np.zeros((6, 2), np.float64)
    for hh in range(6):
        m = np.where(rel > 0, np.exp(lgf[hh] * np.maximum(rel, 0)), 0.0) + \
            np.where(rel < 0, np.exp(lgb[hh] * np.maximum(-rel, 0)), 0.0) + np.where(rel == 0, 2.0, 0.0)
        rmask[:, hh, :] = m
        rpp[:, hh, 0] = np.exp(lgf[hh] * (127 - idx))
        rpp[:, hh, 1] = np.exp(lgb[hh] * idx)
        rxi[:, hh, 0, :] = np.exp(lgf[hh] * (idx + 1))[None]
        rxi[:, hh, 1, :] = np.exp(lgb[hh] * (128 - idx))[None]
        rcd[hh, 0] = np.exp(lgf[hh] * 128)
        rcd[hh, 1] = np.exp(lgb[hh] * 128)
    c["rmask"] = rmask * np.float32(0.125)
    c["rpp"] = rpp
    c["rxi"] = (rxi * np.float32(0.125)).reshape(64, 6 * 2 * 128)
    c["_rcd"] = rcd
    return c


def t5_bucket_np(rel):
    nb = 16
    max_exact = 8
    sign_off = np.where(rel > 0, nb, 0)
    n = np.abs(rel)
    nf = np.maximum(n, 1).astype(np.float32)
    large = max_exact + (np.log(nf / np.float32(max_exact)) / np.float32(math.log(1024 / max_exact))
                         * np.float32(nb - max_exact)).astype(np.int32)
    large = np.minimum(large, nb - 1)
    return sign_off + np.where(n < max_exact, n, large)


def dil_bias_tables(rel_bias):
    out = np.full((128, 3, 4, 4, 256), NEG, np.float32)
    q = np.arange(128)[:, None]
    k = np.arange(256)[None, :]
    rel = k - 64 - q
    band = np.abs(rel) <= 64
    for g, dil in enumerate((1, 4, 16)):
        bk = t5_bucket_np(rel * dil)
        for s in range(4):
            vals = rel_bias[:, g * 4 + s][bk]
            for v in range(4):
                ok = band.copy()
                if v & 1:
                    ok &= (k >= 64)
                if v & 2:
                    ok &= (k < 192)
                out[:, g, s, v, :] = np.where(ok, vals, np.float32(NEG))
    return out.reshape(128, 3 * 4 * 4 * 256)


def build(T, DEPTH, debug=False, stop_after=None, mixers=("hg", "ret", "dil")):
    nc = bass.Bass("TRN2", target_bir_lowering=False)
    NT = T // 128
    NSB = T // 1024
    kind_dbg = "ExternalOutput" if debug else None

    def din(name, shape, dt=F32):
        return nc.dram_tensor(name, list(shape), dt, kind="ExternalInput").ap()

    def dscr(name, shape, dt=F32, out=False):
        if out or debug:
            return nc.dram_tensor(name, list(shape), dt, kind="ExternalOutput").ap()
        return nc.dram_tensor(name, list(shape), dt).ap()

    xT_in = din("xT", [D, T])
    w_in = din("w_in", [DEPTH, D, INW])
    w_out = din("w_out", [DEPTH, D, D])
    w_up = din("w_up", [DEPTH, D, DFF])
    w_down = din("w_down", [DEPTH, DFF, D])
    gmix = din("gmix", [128, DEPTH * 16])
    gmlp = din("gmlp", [128, DEPTH * 16])
    lbf_bc = din("lbf_bc", [128, 4 * 768])
    lbb_bc = din("lbb_bc", [128, 4 * 768])
    hgn_bc = din("hgn_bc", [128, DEPTH * 768])
    rtn_bc = din("rtn_bc", [128, DEPTH * 768])
    qn_bc = din("qn_bc", [128, DEPTH * 128])
    kn_bc = din("kn_bc", [128, DEPTH * 128])
    dbias = din("dbias", [128, 3 * 4 * 4 * 256])
    c_rope = din("rope", [128, NT, 128])
    c_hgm = din("hgm", [128, 6, 128])
    c_ind = din("ind", [128, 4])
    c_ident = din("ident", [128, 128])
    c_rmask = din("rmask", [128, 6, 128])
    c_rpp = din("rpp", [128, 6, 4])
    c_rxi = din("rxi", [64, 6 * 2 * 128])
    outT = dscr("outT", [D, T], out=True)
    NH = DBG.get("NH", 2 if T * INW * 4 > 200e6 else 1)
    HR = T // NH
    hparts = [dscr("h_scr%d" % i, [HR, INW]) for i in range(NH)]

    def hrows(t0, n=128):
        i = t0 // HR
        return hparts[i][t0 - i * HR:t0 - i * HR + n, :]
    yT = dscr("yT_scr", [D, T], BF16)
    og = dscr("og_scr", [3, 4, T, 256])
    rcd = host_consts(128)["_rcd"]

    S = Sched(nc, same_engine_sync=DBG.get('same', True))
    qrr = [0]

    def dq():
        qrr[0] += 1
        return "sp"

    def MM(out, lhsT, rhs, start, stop, r, w):
        return S.pe(lambda e: e.matmul(out, lhsT=lhsT, rhs=rhs, start=start, stop=stop), r, w)

    def TR(out, in_, ident, r, w):
        return S.pe(lambda e: e.transpose(out, in_, ident), r, w)

    def ACT(out, in_, func, r, w, bias=None, scale=None, accum=None):
        kw = {}
        if bias is not None:
            kw["bias"] = bias
        if scale is not None:
            kw["scale"] = scale
        if accum is not None:
            kw["accum_out"] = accum
        return S.act(lambda e: e.activation(out=out, in_=in_, func=func, **kw), r, w)

    def TT(eng, out, in0, in1, op, r, w):
        return S.op(eng, lambda e: e.tensor_tensor(out=out, in0=in0, in1=in1, op=op), r, w)

    def TS(eng, out, in0, s1, s2, op0, op1, r, w):
        if op1 is None:
            return S.op(eng, lambda e: e.tensor_scalar(out=out, in0=in0, scalar1=s1, scalar2=None, op0=op0), r, w)
        return S.op(eng, lambda e: e.tensor_scalar(out=out, in0=in0, scalar1=s1, scalar2=s2, op0=op0, op1=op1), r, w)

    def RECIP(out, in_, r, w):
        return S.dve(lambda e: e.reciprocal(out=out, in_=in_), r, w)

    def STT(eng, out, in0, scalar, in1, op0, op1, r, w):
        return S.op(eng, lambda e: e.scalar_tensor_tensor(out=out, in0=in0, scalar=scalar, in1=in1, op0=op0, op1=op1), r, w)

    def CP(eng, out, in_, r, w):
        if eng == "act":
            return S.act(lambda e: e.copy(out=out, in_=in_), r, w)
        return S.op(eng, lambda e: e.tensor_copy(out=out, in_=in_), r, w)

    def DMA(q, out, in_, r, w):
        return S.dma(q, lambda e: e.dma_start(out=out, in_=in_), r, w)

    def MEMSET(eng, ap, val, w):
        return S.op(eng, lambda e: e.memset(ap, val), (), w)

    final_ops = []

    def dense_stage(l):
        with contextlib.ExitStack() as st:
            def sb(name, shape, dt):
                return st.enter_context(nc.sbuf_tensor("ds%d_%s" % (l, name), shape, dt))

            def ps(name, shape, dt):
                return st.enter_context(nc.psum_tensor("dsp%d_%s" % (l, name), shape, dt))

            XT = sb("XT", [128, 16, 1024], F32)
            AB = sb("AB", [128, 16, 1024], BF16)
            WB = [sb("WB%d" % i, [128, 16, 512], BF16) for i in range(2)]
            WD = [sb("WD%d" % i, [128, 4, 2048], BF16) for i in range(2)]
            AT = sb("AT", [128, 4, 1024], BF16)
            RS = sb("RS", [128, 1024], F32)
            SQ = [sb("SQ%d" % i, [128, 512], F32) for i in range(2)]
            RL = [sb("RL%d" % i, [128, 512], F32) for i in range(2)]
            HS = [sb("HS%d" % i, [128, 512], F32) for i in range(3)]
            ONES = sb("ONES", [128, 128], F32)
            G1 = sb("G1", [128, DEPTH * 16], F32)
            G2 = sb("G2", [128, DEPTH * 16], F32)
            PB = [ps("PB%d" % i, [128, 512], F32) for i in range(6)]
            PSTAT = [ps("PS%d" % i, [128, 512], F32) for i in range(2)]
            MEMSET("pool", ONES[:], 1.0, ["ONES"])
            DMA("sp", G1[:], gmix, (), ["G1"])
            DMA("sp", G2[:], gmlp, (), ["G2"])
            cnt = dict(pb=0, wb=0, wd=0, sq=0, rl=0, hs=0, ev=0)

            def next_pb():
                cnt["pb"] += 1
                return cnt["pb"] % 6

            def load_wb(src):
                i = cnt["wb"] % 2
                cnt["wb"] += 1
                v = src.rearrange("(c p) n -> p c n", p=128)
                for q4 in range(4):
                    DMA("pool", WB[i][:, q4 * 4:(q4 + 1) * 4, :], v[:, q4 * 4:(q4 + 1) * 4, :], (), ["WB%d_%d" % (i, q4)])
                return i

            def load_wd(src):
                i = cnt["wd"] % 2
                cnt["wd"] += 1
                v = src.rearrange("(c p) n -> p c n", p=128)
                for q4 in range(4):
                    DMA("pool", WD[i][:, q4:q4 + 1, :], v[:, q4:q4 + 1, :], (), ["WD%d_%d" % (i, q4)])
                return i

            def norm_to_AB(gtile, goff):
                for th in range(2):
                    pst = PSTAT[th]
                    for c in range(16):
                        i = cnt["sq"] % 2
                        cnt["sq"] += 1
                        ACT(SQ[i][:], XT[:, c, th * 512:(th + 1) * 512], AF.Square, ["XT%d" % c], ["SQ%d" % i])
                        MM(pst[:], ONES[:], SQ[i][:], c == 0, c == 15, ["ONES", "SQ%d" % i], ["PST%d" % th])
                    sl = RS[:, th * 512:(th + 1) * 512]
                    ACT(sl, pst[:], AF.Sqrt, ["PST%d" % th], ["RS%d" % th], bias=EPS, scale=1.0 / D)
                    RECIP(sl, sl, ["RS%d" % th], ["RS%d" % th])
                for c in range(16):
                    STT("dve", AB[:, c, :], XT[:, c, :], gtile[:, goff + c:goff + c + 1], RS[:], ALU.mult, ALU.mult,
                        ["XT%d" % c, "RS0", "RS1", "G1", "G2"], ["AB%d" % c])

            for sbi in range(NSB):
                t0 = sbi * 1024
                src = xT_in if l <= 1 else outT
                for c in range(16):
                    DMA(dq(), XT[:, c, :], src[c * 128:(c + 1) * 128, t0:t0 + 1024], (), ["XT%d" % c])
                if l > 0:
                    lp = l - 1
                    for c in range(16):
                        DMA(dq(), AB[:, c, :], yT[c * 128:(c + 1) * 128, t0:t0 + 1024], (), ["AB%d" % c])
                    abk = ["AB%d" % c for c in range(16)]
                    for dg in range(4):
                        wi = load_wb(w_out[lp, :, dg * 512:(dg + 1) * 512])
                        for dc in range(4):
                            dch = dg * 4 + dc
                            for th in range(2):
                                pb = next_pb()
                                for k in range(16):
                                    MM(PB[pb][:], WB[wi][:, k, dc * 128:(dc + 1) * 128], AB[:, k, th * 512:(th + 1) * 512],
                                       k == 0, k == 15, ["WB%d_%d" % (wi, k // 4), "AB%d" % k], ["PB%d" % pb])
                                xs = XT[:, dch, th * 512:(th + 1) * 512]
                                TT("dve", xs, xs, PB[pb][:], ALU.add, ["PB%d" % pb, "XT%d" % dch], ["XT%d" % dch])
                    norm_to_AB(G2, lp * 16)
                    for fg in range(16):
                        wi = load_wb(w_up[lp, :, fg * 512:(fg + 1) * 512])
                        wdi = load_wd(w_down[lp, fg * 512:(fg + 1) * 512, :])
                        for fc in range(4):
                            for th in range(2):
                                pb = next_pb()
                                for k in range(16):
                                    MM(PB[pb][:], WB[wi][:, k, fc * 128:(fc + 1) * 128], AB[:, k, th * 512:(th + 1) * 512],
                                       k == 0, k == 15, ["WB%d_%d" % (wi, k // 4), "AB%d" % k], ["PB%d" % pb])
                                i = cnt["rl"] % 2
                                cnt["rl"] += 1
                                ACT(RL[i][:], PB[pb][:], AF.Relu, ["PB%d" % pb], ["RL%d" % i])
                                ACT(AT[:, fc, th * 512:(th + 1) * 512], RL[i][:], AF.Square, ["RL%d" % i], ["AT%d_%d" % (fc, th)])
                        for dch in range(16):
                            for th in range(2):
                                pb = next_pb()
                                for fc in range(4):
                                    MM(PB[pb][:], WD[wdi][:, fc, dch * 128:(dch + 1) * 128], AT[:, fc, th * 512:(th + 1) * 512],
                                       fc == 0, fc == 3, ["WD%d_%d" % (wdi, fc), "AT%d_%d" % (fc, th)], ["PB%d" % pb])
                                xs = XT[:, dch, th * 512:(th + 1) * 512]
                                TT("dve", xs, xs, PB[pb][:], ALU.add, ["PB%d" % pb, "XT%d" % dch], ["XT%d" % dch])
                    for c in range(16):
                        o = DMA(dq(), outT[c * 128:(c + 1) * 128, t0:t0 + 1024], XT[:, c, :], ["XT%d" % c], ())
                        if l == DEPTH:
                            final_ops.append(o)
                if l < DEPTH:
                    norm_to_AB(G1, l * 16)
                    for cg in range(21):
                        wi = load_wb(w_in[l, :, cg * 512:(cg + 1) * 512])
                        for tt in range(8):
                            pb = next_pb()
                            for k in range(16):
                                MM(PB[pb][:], AB[:, k, tt * 128:(tt + 1) * 128], WB[wi][:, k, :],
                                   k == 0, k == 15, ["WB%d_%d" % (wi, k // 4), "AB%d" % k], ["PB%d" % pb])
                            i = cnt["hs"] % 3
                            cnt["hs"] += 1
                            cnt["ev"] += 1
                            CP("act" if cnt["ev"] % 2 else "dve", HS[i][:], PB[pb][:], ["PB%d" % pb], ["HS%d" % i])
                            DMA(dq(), hrows(t0 + tt * 128)[:, cg * 512:(cg + 1) * 512], HS[i][:], ["HS%d" % i], ())
        S.barrier()

    from types import SimpleNamespace
    ctx = SimpleNamespace(**locals())
    for l in range(DEPTH + 1):
        dense_stage(l)
        if stop_after == "dense0":
            break
        if l < DEPTH:
            if "hg" in mixers:
                mixer_hg(ctx, l)
            if "ret" in mixers:
                mixer_ret(ctx, l)
            if "dil" in mixers:
                mixer_dil(ctx, l)
            if stop_after == "mix0":
                break
    if not final_ops:
        final_ops.append(S.dma("sp", lambda e: e.dma_start(out=outT[0:128, 0:128], in_=xT_in[0:128, 0:128]), (), ()))
    S.emit(final_wait_ops=final_ops)
    return nc, S


def run_streams(gens):
    gens = list(gens)
    while gens:
        for g in list(gens):
            try:
                next(g)
            except StopIteration:
                gens.remove(g)


def mixer_hg(ctx, l):
    nc, S, T, NT = ctx.nc, ctx.S, ctx.T, ctx.NT
    MM, TR, ACT, TT, TS, STT, CP, DMA, MEMSET, RECIP = (ctx.MM, ctx.TR, ctx.ACT, ctx.TT, ctx.TS, ctx.STT, ctx.CP,
                                                        ctx.DMA, ctx.MEMSET, ctx.RECIP)
    yT = ctx.yT
    with contextlib.ExitStack() as st0:
        OML = [st0.enter_context(nc.sbuf_tensor("hg%d_OML%d" % (l, d), [128, 768], F32)) for d in range(2)]
        with contextlib.ExitStack() as st:
            def sb(name, shape, dt):
                return st.enter_context(nc.sbuf_tensor("hgl%d_%s" % (l, name), shape, dt))
            LBR = sb("LBR", [128, 4 * 768], F32)
            LBT = sb("LBT", [128, 768], F32)
            LBX = sb("LBX", [128, 768], F32)
            for d, src in enumerate((ctx.lbf_bc, ctx.lbb_bc)):
                if l == 0:
                    MEMSET("dve", OML[d][:], 1.0, ["OML%d" % d])
                    continue
                DMA("sp", LBR[:], src, (), ["LBR"])
                L4 = [LBR[:, i * 768:(i + 1) * 768] for i in range(4)]
                TT("dve", LBT[:], L4[0], L4[1], ALU.max, ["LBR"], ["LBT"])
                TT("dve", LBT[:], LBT[:], L4[2], ALU.max, ["LBR", "LBT"], ["LBT"])
                TT("dve", LBT[:], LBT[:], L4[3], ALU.max, ["LBR", "LBT"], ["LBT"])
                for i in range(4):
                    TT("dve", L4[i], L4[i], LBT[:], ALU.subtract, ["LBR", "LBT"], ["LBR"])
                ACT(LBR[:], LBR[:], AF.Exp, ["LBR"], ["LBR"])
                TT("dve", LBT[:], L4[0], L4[1], ALU.add, ["LBR"], ["LBT"])
                TT("dve", LBT[:], LBT[:], L4[2], ALU.add, ["LBR", "LBT"], ["LBT"])
                TT("dve", LBT[:], LBT[:], L4[3], ALU.add, ["LBR", "LBT"], ["LBT"])
                RECIP(LBT[:], LBT[:], ["LBT"], ["LBT"])
                CP("dve", LBX[:], L4[1], ["LBR"], ["LBX"])
                for i in range(2, l + 1):
                    TT("dve", LBX[:], LBX[:], L4[i], ALU.add, ["LBR", "LBX"], ["LBX"])
                TT("dve", LBX[:], LBX[:], LBT[:], ALU.mult, ["LBT", "LBX"], ["LBX"])
                TS("dve", OML[d][:], LBX[:], -1.0, 1.0, ALU.mult, ALU.add, ["LBX"], ["OML%d" % d])
        S.barrier()
        with contextlib.ExitStack() as st:
            def sb(name, shape, dt):
                return st.enter_context(nc.sbuf_tensor("hg%d_%s" % (l, name), shape, dt))

            def ps(name, shape, dt):
                return st.enter_context(nc.psum_tensor("hgp%d_%s" % (l, name), shape, dt))

            HGM = sb("HGM", [128, 6, 128], F32)
            IND = sb("IND", [128, 4], F32)
            IDB = sb("IDB", [128, 128], F32)
            HGN = sb("HGN", [128, 768], F32)
            DMA("sp", HGM[:], ctx.c_hgm, (), ["HGM"])
            DMA("sp", IND[:], ctx.c_ind, (), ["IND"])
            DMA("sp", IDB[:], ctx.c_ident, (), ["IDB"])
            DMA("sp", HGN[:], ctx.hgn_bc[:, l * 768:(l + 1) * 768], (), ["HGN"])
            NSTR = 3
            NRING = 8
            PZ = [ps("PZ_%d" % z, [128, 512], F32) for z in range(2)]
            ST = []
            for si in range(NSTR):
                X = dict()
                X["OF"] = sb("OF_%d" % si, [128, NT, 128], F32)
                X["YS"] = [sb("YS%d_%d" % (p, si), [128, 256], BF16) for p in range(2)]
                X["IN"] = [sb("IN%d_%d" % (p, si), [128, 5, 128], F32) for p in range(2)]
                for nm in ("SIGN", "LOGF", "KK", "EB", "ENB", "EBR", "KH", "O", "GSG", "QT", "KT", "Y"):
                    X[nm] = [sb("%s_%d" % (nm, si), [128, 128], F32)]
                for nm in ("VB", "QTT", "KTT", "AM"):
                    X[nm] = [sb("%s_%d" % (nm, si), [128, 128], BF16)]
                X["KHZ"] = [sb("KHZ_%d" % si, [128, 4, 128], BF16)]
                X["QZ"] = [sb("QZ_%d" % si, [128, 640], F32)]
                X["FD"] = [sb("FD_%d" % si, [128, 4], F32)]
                X["SS"] = [sb("SS_%d" % si, [128, 2], F32)]
                X["S32"] = [sb("S32_%d_%d" % (p, si), [128, 128], F32) for p in range(NRING)]
                X["PA"] = ps("PA_%d" % si, [128, 512], F32)
                X["PKV"] = ps("PKV_%d" % si, [128, 4, 128], F32)
                X["PT"] = PZ[si % 2]
                MEMSET("pool", X["QZ"][0][:], 0.0, ["QZ0_%d" % si])
                ST.append(X)

            def stream(si, hd):
                X = ST[si]
                hs = slice(hd * 128, (hd + 1) * 128)
                cnt = dict(p=0, r=0)
                kb = lambda nm: ("PT_%d" % (si % 2)) if nm == "PT" else ("PA_%d" % si if nm in ("PA", "PO") else "%s_%d" % (nm, si))
                pa, pkv, pt = X["PA"], X["PKV"], X["PT"]
                po = pa
                for d in range(2):
                    order = list(range(NT)) if d == 0 else list(range(NT - 1, -1, -1))
                    corder = [0, 1, 2, 3] if d == 0 else [3, 2, 1, 0]
                    cur = cnt["r"] % NRING
                    cnt["r"] += 1
                    MEMSET("dve", X["S32"][cur][:], 0.0, ["S32_%d_%d" % (cur, si)])
                    for t in order:
                        pi = cnt["p"] % 2
                        cnt["p"] += 1
                        p = 0
                        P = lambda nm: X[nm][0][:]
                        K_ = lambda nm, pi=pi: ("IN%d_%d" % (pi, si)) if nm == "IN" else "%s0_%d" % (nm, si)
                        IN = X["IN"][pi]
                        t0 = t * 128
                        src = ctx.hrows(t0)[:, 0:3840].rearrange("t (s c) -> t s c", s=5)[:, :, hs]
                        DMA("sp", IN[:], src, (), [K_("IN")])
                        ACT(P("SIGN"), IN[:, 2 + d, :], AF.Sigmoid, [K_("IN")], [K_("SIGN")], scale=-1.0)
                        CP("pool", P("VB"), IN[:, 1, :], [K_("IN")], [K_("VB")])
                        yield
                        TT("dve", P("KK"), P("SIGN"), OML[d][:, hs], ALU.mult, [K_("SIGN"), "OML%d" % d], [K_("KK")])
                        ACT(P("LOGF"), P("KK"), AF.Ln, [K_("KK")], [K_("LOGF")], scale=-1.0, bias=1.0)
                        MM(pa[:, 0:128], HGM[:, 2 * d, :], P("LOGF"), True, True, ["HGM", K_("LOGF")], [kb("PA")])
                        MM(pa[:, 128:256], HGM[:, 2 * d + 1, :], P("LOGF"), True, True, ["HGM", K_("LOGF")], [kb("PA")])
                        MM(pt[:, 384:388], P("LOGF"), IND[:], True, True, ["IND", K_("LOGF")], [kb("PT")])
                        yield
                        ACT(P("EB"), pa[:, 0:128], AF.Exp, [kb("PA")], [K_("EB")])
                        ACT(P("ENB"), pa[:, 0:128], AF.Exp, [kb("PA")], [K_("ENB")], scale=-1.0)
                        ACT(P("EBR"), pa[:, 128:256], AF.Exp, [kb("PA")], [K_("EBR")])
                        ACT(X["FD"][p][:], pt[:, 384:388], AF.Exp, [kb("PT")], [K_("FD")])
                        TT("dve", P("QT"), IN[:, 0, :], P("EB"), ALU.mult, [K_("IN"), K_("EB")], [K_("QT")])
                        TT("dve", P("KT"), P("KK"), P("ENB"), ALU.mult, [K_("KK"), K_("ENB")], [K_("KT")])
                        TT("dve", P("KH"), P("KK"), P("EBR"), ALU.mult, [K_("KK"), K_("EBR")], [K_("KH")])
                        for c in range(4):
                            TS("dve" if c < 2 else "pool", X["KHZ"][p][:, c, :], P("KH"), IND[:, c:c + 1], None, ALU.mult, None,
                               [K_("KH"), "IND"], [K_("KHZ")])
                        yield
                        TR(pt[:, 0:128], P("QT"), IDB[:], [K_("QT"), "IDB"], [kb("PT")])
                        TR(pt[:, 128:256], P("KT"), IDB[:], [K_("KT"), "IDB"], [kb("PT")])
                        CP("act", P("QTT"), pt[:, 0:128], [kb("PT")], [K_("QTT")])
                        CP("act", P("KTT"), pt[:, 128:256], [kb("PT")], [K_("KTT")])
                        CP("dve", X["QZ"][p][:].rearrange("p (c x) -> p c x", x=160)[:, :, 0:32],
                           pt[:, 0:128].rearrange("p (c x) -> p c x", x=32), [kb("PT")], [K_("QZ")])
                        for c in range(4):
                            MM(pkv[:, c, :], X["KHZ"][p][:, c, :], P("VB"), True, True, [K_("KHZ"), K_("VB")], [kb("PKV")])
                        yield
                        MM(pa[:, 256:384], P("KTT"), P("QTT"), True, True, [K_("KTT"), K_("QTT")], [kb("PA")])
                        TT("dve", P("AM"), pa[:, 256:384], HGM[:, 4 + d, :], ALU.mult, [kb("PA"), "HGM"], [K_("AM")])
                        states = [cur]
                        for c in corder:
                            nxt = cnt["r"] % NRING
                            cnt["r"] += 1
                            STT("dve", X["S32"][nxt][:], X["S32"][cur][:], X["FD"][p][:, c:c + 1], pkv[:, c, :], ALU.mult, ALU.add,
                                ["S32_%d_%d" % (cur, si), K_("FD"), kb("PKV")], ["S32_%d_%d" % (nxt, si)])
                            cur = nxt
                            states.append(cur)
                        yield
                        MM(po[:, 384:512], P("AM"), P("VB"), True, False, [K_("AM"), K_("VB")], [kb("PO")])
                        for ci, c in enumerate(corder):
                            sc = states[ci]
                            MM(po[:, 384:512], X["QZ"][p][:, 128 * c:128 * c + 128], X["S32"][sc][:], False, ci == 3,
                               [K_("QZ"), "S32_%d_%d" % (sc, si)], [kb("PO")])
                        if d == 0:
                            CP("act", X["OF"][:, t, :], po[:, 384:512], [kb("PO")], ["OF%d_%d" % (t, si)])
                        else:
                            SSp = X["SS"][p]
                            TT("dve", P("O"), X["OF"][:, t, :], po[:, 384:512], ALU.add, ["OF%d_%d" % (t, si), kb("PO")], [K_("O")])
                            ACT(P("KH"), P("O"), AF.Square, [K_("O")], [K_("KH"), K_("SS")], accum=SSp[:, 0:1])
                            ACT(SSp[:, 1:2], SSp[:, 0:1], AF.Sqrt, [K_("SS")], [K_("SS")], bias=EPS, scale=1.0 / 128)
                            RECIP(SSp[:, 1:2], SSp[:, 1:2], [K_("SS")], [K_("SS")])
                            ACT(P("GSG"), IN[:, 4, :], AF.Silu, [K_("IN")], [K_("GSG")])
                            TT("pool", P("GSG"), P("GSG"), HGN[:, hs], ALU.mult, [K_("GSG"), "HGN"], [K_("GSG")])
                            STT("dve", P("Y"), P("O"), SSp[:, 1:2], P("GSG"), ALU.mult, ALU.mult,
                                [K_("O"), K_("SS"), K_("GSG")], [K_("Y")])
                            TR(pt[:, 256:384], P("Y"), IDB[:], [K_("Y"), "IDB"], [kb("PT")])
                            yp = (t // 2) % 2
                            CP("act", X["YS"][yp][:, (t % 2) * 128:(t % 2) * 128 + 128], pt[:, 256:384], [kb("PT")], ["YS%d_%d" % (yp, si)])
                            if t % 2 == 0:
                                DMA("sp", yT[hd * 128:(hd + 1) * 128, t0:t0 + 256], X["YS"][yp][:], ["YS%d_%d" % (yp, si)], ())
                        yield

            for h0 in range(0, 6, NSTR):
                run_streams([stream(si, h0 + si) for si in range(NSTR)])
    S.barrier()


def mixer_ret(ctx, l):
    nc, S, T, NT = ctx.nc, ctx.S, ctx.T, ctx.NT
    MM, TR, ACT, TT, TS, STT, CP, DMA, MEMSET, RECIP = (ctx.MM, ctx.TR, ctx.ACT, ctx.TT, ctx.TS, ctx.STT, ctx.CP,
                                                        ctx.DMA, ctx.MEMSET, ctx.RECIP)
    yT, rcd = ctx.yT, ctx.rcd
    with contextlib.ExitStack() as st:
        def sb(name, shape, dt):
            return st.enter_context(nc.sbuf_tensor("rt%d_%s" % (l, name), shape, dt))

        def ps(name, shape, dt):
            return st.enter_context(nc.psum_tensor("rtp%d_%s" % (l, name), shape, dt))

        RMASK = sb("RMASK", [128, 6, 128], F32)
        RPP = sb("RPP", [128, 6, 4], F32)
        RXI = sb("RXI", [64, 6 * 2 * 128], F32)
        ROPE = sb("ROPE", [128, NT, 128], F32)
        IDB = sb("IDB", [128, 128], F32)
        RTN = sb("RTN", [128, 768], F32)
        OF = sb("OF", [128, NT, 128], F32)
        QTB = sb("QTB", [64, NT, 128], F32)
        KVB = sb("KVB", [64, NT, 128], F32)
        YTH = sb("YTH", [128, T], BF16)
        INQK = [sb("INQK%d" % p, [128, 2, 64], F32) for p in range(4)]
        ING = [sb("ING%d" % p, [128, 128], F32) for p in range(4)]
        VB = [sb("VB%d" % p, [128, 128], BF16) for p in range(4)]
        W = {}
        for nm in ("A", "B", "RK"):
            W[nm] = [sb("%s%d" % (nm, p), [128, 64], F32) for p in range(4)]
        for nm in ("KZF", "KZB"):
            W[nm] = [sb("%s%d" % (nm, p), [128, 64], BF16) for p in range(4)]
        for nm in ("RQb", "RKb"):
            W[nm] = [sb("%s%d" % (nm, p), [128, 64], F32) for p in range(4)]
        for nm in ("QT", "KT"):
            W[nm] = [sb("%s%d" % (nm, p), [64, 128], BF16) for p in range(4)]
        W["QXF"] = [sb("QXF%d" % p, [64, 128], F32) for p in range(4)]
        for nm in ("PM",):
            W[nm] = [sb("%s%d" % (nm, p), [128, 128], BF16) for p in range(4)]
        for nm in ("O", "GSG", "JK", "Y"):
            W[nm] = [sb("%s%d" % (nm, p), [128, 128], F32) for p in range(4)]
        SS = [sb("SS%d" % p, [128, 4], F32) for p in range(4)]
        NR = 8
        SF32 = [sb("SF32_%d" % p, [64, 128], F32) for p in range(NR)]
        BK1 = [ps("BK1_%d" % p, [128, 512], F32) for p in range(4)]
        BK2 = [ps("BK2_%d" % p, [128, 512], F32) for p in range(4)]
        DMA("sp", RMASK[:], ctx.c_rmask, (), ["RMASK"])
        DMA("sp", RPP[:], ctx.c_rpp, (), ["RPP"])
        DMA("sp", RXI[:], ctx.c_rxi, (), ["RXI"])
        DMA("sp", ROPE[:], ctx.c_rope, (), ["ROPE"])
        DMA("sp", IDB[:], ctx.c_ident, (), ["IDB"])
        DMA("sp", RTN[:], ctx.rtn_bc[:, l * 768:(l + 1) * 768], (), ["RTN"])
        NL = 4
        for hd in range(6):
            hs = slice(hd * 128, (hd + 1) * 128)
            xo = hd * 256

            def fwd_lane(lane, hd=hd, hs=hs, xo=xo):
                p = lane
                P = lambda nm: W[nm][p][:]
                K_ = lambda nm: "%s%d" % (nm, p)
                for t in range(lane, NT, NL):
                    cur, nxt = t % NR, (t + 1) % NR
                    t0 = t * 128
                    src = ctx.hrows(t0)[:, RT_Q:RT_Q + 768].rearrange("t (s c) -> t s c", s=2)[:, :, hd * 64:(hd + 1) * 64]
                    DMA("sp", INQK[p][:], src, (), [K_("INQK")])
                    DMA("pool", VB[p][:], ctx.hrows(t0)[:, RT_V + hd * 128:RT_V + (hd + 1) * 128], (), [K_("VB")])
                    CC = ROPE[:, t, 0:64]
                    for si, dst in ((0, P("RQb")), (1, P("RK"))):
                        x = INQK[p][:, si, :]
                        TT("dve", P("A"), x, CC, ALU.mult, [K_("INQK"), "ROPE"], [K_("A")])
                        TT("dve", W["B"][p][:, 0:32], INQK[p][:, si, 32:64], ROPE[:, t, 64:96], ALU.mult, [K_("INQK"), "ROPE"], [K_("B")])
                        TT("dve", W["B"][p][:, 32:64], INQK[p][:, si, 0:32], ROPE[:, t, 96:128], ALU.mult, [K_("INQK"), "ROPE"], [K_("B")])
                        TT("dve", dst, P("A"), P("B"), ALU.add, [K_("A"), K_("B")], [K_("RQb") if si == 0 else K_("RK")])
                    ACT(P("KZF"), P("RK"), AF.Copy, [K_("RK"), "RPP"], [K_("KZF")], scale=RPP[:, hd, 0:1])
                    ACT(P("KZB"), P("RK"), AF.Copy, [K_("RK"), "RPP"], [K_("KZB")], scale=RPP[:, hd, 1:2])
                    yield
                    pt = BK2[p]
                    pkv = BK2[p][0:64, 256:512].rearrange("p (a b) -> p a b", a=2)
                    TR(pt[0:64, 0:128], P("RQb"), IDB[:], [K_("RQb"), "IDB"], [K_("PT")])
                    TR(pt[0:64, 128:256], P("RK"), IDB[:], [K_("RK"), "IDB"], [K_("PT")])
                    MM(pkv[:, 0, :], P("KZF"), VB[p][:], True, True, [K_("KZF"), K_("VB")], [K_("PT")])
                    MM(pkv[:, 1, :], P("KZB"), VB[p][:], True, True, [K_("KZB"), K_("VB")], [K_("PT")])
                    yield
                    CP("act", P("QT"), pt[0:64, 0:128], [K_("PT")], [K_("QT")])
                    CP("act", P("KT"), pt[0:64, 128:256], [K_("PT")], [K_("KT")])
                    TT("dve", P("QXF"), pt[0:64, 0:128], RXI[:, xo:xo + 128], ALU.mult, [K_("PT"), "RXI"], [K_("QXF")])
                    TT("dve", QTB[:, t, :], pt[0:64, 0:128], RXI[:, xo + 128:xo + 256], ALU.mult, [K_("PT"), "RXI"], ["QTB%d" % t])
                    STT("dve", SF32[nxt][:], SF32[cur][:], float(rcd[hd, 0]), pkv[:, 0, :], ALU.mult, ALU.add,
                        ["SF32_%d" % cur, K_("PT")], ["SF32_%d" % nxt])
                    CP("act", KVB[:, t, :], pkv[:, 1, :], [K_("PT")], ["KVB%d" % t])
                    yield
                    pa = BK1[p]
                    MM(pa[:, 0:128], P("KT"), P("QT"), True, True, [K_("KT"), K_("QT")], [K_("PA")])
                    yield
                    TT("dve", P("PM"), pa[:, 0:128], RMASK[:, hd, :], ALU.mult, [K_("PA"), "RMASK"], [K_("PM")])
                    yield
                    po = BK1[p]
                    MM(po[:, 128:256], P("PM"), VB[p][:], True, False, [K_("PM"), K_("VB")], [K_("PA")])
                    MM(po[:, 128:256], P("QXF"), SF32[cur][:], False, True, [K_("QXF"), "SF32_%d" % cur], [K_("PA")])
                    yield
                    CP("act", OF[:, t, :], po[:, 128:256], [K_("PA")], ["OF%d" % t])

            MEMSET("dve", SF32[0][:], 0.0, ["SF32_0"])
            run_streams([fwd_lane(i) for i in range(NL)])

            def bwd_lane(lane, hd=hd, hs=hs):
                p = lane
                P = lambda nm: W[nm][p][:]
                K_ = lambda nm: "%s%d" % (nm, p)
                for idx in range(lane, NT, NL):
                    t = NT - 1 - idx
                    cur, nxt = idx % NR, (idx + 1) % NR
                    t0 = t * 128
                    DMA("sp", ING[p][:], ctx.hrows(t0)[:, RT_G + hd * 128:RT_G + (hd + 1) * 128], (), [K_("ING")])
                    STT("dve", SF32[nxt][:], SF32[cur][:], float(rcd[hd, 1]), KVB[:, t, :], ALU.mult, ALU.add,
                        ["SF32_%d" % cur, "KVB%d" % t], ["SF32_%d" % nxt])
                    po = BK1[p]
                    MM(po[:, 128:256], QTB[:, t, :], SF32[cur][:], True, True, ["QTB%d" % t, "SF32_%d" % cur], [K_("PA")])
                    ACT(P("GSG"), ING[p][:], AF.Silu, [K_("ING")], [K_("GSG")])
                    TT("pool", P("GSG"), P("GSG"), RTN[:, hs], ALU.mult, [K_("GSG"), "RTN"], [K_("GSG")])
                    yield
                    TT("dve", P("O"), OF[:, t, :], po[:, 128:256], ALU.add, ["OF%d" % t, K_("PA")], [K_("O")])
                    ACT(P("JK"), P("O"), AF.Identity, [K_("O")], [K_("JK"), K_("SS")], accum=SS[p][:, 0:1])
                    yield
                    TS("dve", SS[p][:, 1:2], SS[p][:, 0:1], -1.0 / 128, None, ALU.mult, None, [K_("SS")], [K_("SS")])
                    ACT(P("JK"), P("O"), AF.Square, [K_("O"), K_("SS")], [K_("JK"), K_("SS")], bias=SS[p][:, 1:2], accum=SS[p][:, 2:3])
                    ACT(SS[p][:, 3:4], SS[p][:, 2:3], AF.Sqrt, [K_("SS")], [K_("SS")], bias=EPS, scale=1.0 / 128)
                    yield
                    RECIP(SS[p][:, 3:4], SS[p][:, 3:4], [K_("SS")], [K_("SS")])
                    TS("dve", P("O"), P("O"), SS[p][:, 1:2], SS[p][:, 3:4], ALU.add, ALU.mult, [K_("O"), K_("SS")], [K_("O")])
                    TT("dve", P("Y"), P("O"), P("GSG"), ALU.mult, [K_("O"), K_("GSG")], [K_("Y")])
                    pt = BK1[p]
                    TR(pt[:, 256:384], P("Y"), IDB[:], [K_("Y"), "IDB"], [K_("PA")])
                    yield
                    CP("act", YTH[:, t0:t0 + 128], pt[:, 256:384], [K_("PA")], ["YTH"])

            MEMSET("dve", SF32[0][:], 0.0, ["SF32_0"])
            run_streams([bwd_lane(i) for i in range(NL)])
            DMA("sp", yT[768 + hd * 128:768 + (hd + 1) * 128, :], YTH[:], ["YTH"], ())
    S.barrier()


def mixer_dil(ctx, l):
    nc, S, T, NT = ctx.nc, ctx.S, ctx.T, ctx.NT
    MM, TR, ACT, TT, TS, STT, CP, DMA, MEMSET, RECIP = (ctx.MM, ctx.TR, ctx.ACT, ctx.TT, ctx.TS, ctx.STT, ctx.CP,
                                                        ctx.DMA, ctx.MEMSET, ctx.RECIP)
    yT, og = ctx.yT, ctx.og
    with contextlib.ExitStack() as st:
        def sb(name, shape, dt):
            return st.enter_context(nc.sbuf_tensor("dl%d_%s" % (l, name), shape, dt))

        def ps(name, shape, dt):
            return st.enter_context(nc.psum_tensor("dlp%d_%s" % (l, name), shape, dt))

        KTW = T + 128 * 16
        NVT = T // 128 + 16
        DB = sb("DB", [128, 3 * 4 * 4 * 256], F32)
        IDB = sb("IDB", [128, 128], F32)
        QG = sb("QG", [128, 128], F32)
        KG = sb("KG", [128, 128], F32)
        QTA = sb("QTA", [128, T], BF16)
        KTP = sb("KTP", [128, KTW], BF16)
        VBA = sb("VBA", [128, NVT, 128], BF16)
        NL = 4
        INQK = [sb("INQK%d" % p, [128, 2, 128], F32) for p in range(NL)]
        QN = [sb("QN%d" % p, [128, 2, 128], F32) for p in range(NL)]
        JK = [sb("JK%d" % p, [128, 128], F32) for p in range(NL)]
        SS = [sb("SS%d" % p, [128, 8], F32) for p in range(NL)]
        SB_ = [sb("SB%d" % p, [128, 256], F32) for p in range(NL)]
        PB_ = [sb("PB%d" % p, [128, 256], F32) for p in range(NL)]
        PTS = [sb("PTS%d" % p, [128, 2, 128], BF16) for p in range(NL)]
        OS = [sb("OS%d" % p, [128, 256], F32) for p in range(NL)]
        PS_ = [ps("PS%d" % p, [128, 512], F32) for p in range(NL)]
        PT = [ps("PT%d" % p, [128, 512], F32) for p in range(NL)]
        DMA("sp", DB[:], ctx.dbias, (), ["DB"])
        DMA("sp", IDB[:], ctx.c_ident, (), ["IDB"])
        DMA("sp", QG[:], ctx.qn_bc[:, l * 128:(l + 1) * 128], (), ["QG"])
        DMA("sp", KG[:], ctx.kn_bc[:, l * 128:(l + 1) * 128], (), ["KG"])
        TS("dve", QG[:], QG[:], 128.0 ** -0.5, None, ALU.mult, None, ["QG"], ["QG"])
        for p in range(NL):
            MEMSET("pool", OS[p][:], 0.0, ["OS%d" % p])
        for g, dil in enumerate((1, 4, 16)):
            L = T // dil
            nblk = L // 128
            nvt = nblk + 1
            lw = 128 // dil
            for s in range(4):
                base = DL0 + g * 1536
                MEMSET("pool", KTP[:], 0.0, ["KTP"])
                MEMSET("pool", VBA[:], 0.0, ["VBAz"] + ["VBA_t%d" % i for i in range(NVT)])
                vcol = base + 1024 + s * 128
                for r in range(dil):
                    for m in range(nvt):
                        lo = max(128 * m - 64, 0)
                        hi = min(128 * m + 64, L)
                        p0 = lo - (128 * m - 64)
                        n = hi - lo
                        i0 = 0
                        while i0 < n:
                            row0 = (lo + i0) * dil + r
                            part = row0 // ctx.HR
                            lim = (part + 1) * ctx.HR
                            n1 = min(n - i0, (lim - row0 + dil - 1) // dil)
                            lr = row0 - part * ctx.HR
                            rows = ctx.hparts[part][lr:lr + (n1 - 1) * dil + 1:dil, vcol:vcol + 128]
                            DMA("pool", VBA[p0 + i0:p0 + i0 + n1, r * nvt + m, :], rows, ["VBAz"], ["VBA_t%d" % (r * nvt + m)])
                            i0 += n1
                def prep_lane(lane, g=g, dil=dil, L=L, lw=lw, s=s, base=base):
                    p = lane
                    K_ = lambda nm: "%s%d" % (nm, p)
                    for t in range(lane, NT, NL):
                        t0 = t * 128
                        src = ctx.hrows(t0)[:, base:base + 1024].rearrange("t (s c) -> t s c", s=2)[:, :, s * 128:(s + 1) * 128]
                        DMA("sp", INQK[p][:], src, (), [K_("INQK")])
                        ACT(JK[p][:], INQK[p][:, 0, :], AF.Square, [K_("INQK")], [K_("JK"), K_("SSa")], accum=SS[p][:, 0:1])
                        ACT(JK[p][:], INQK[p][:, 1, :], AF.Square, [K_("INQK")], [K_("JK"), K_("SSb")], accum=SS[p][:, 1:2])
                        ACT(SS[p][:, 2:4], SS[p][:, 0:2], AF.Sqrt, [K_("SSa"), K_("SSb")], [K_("SSr")], bias=EPS, scale=1.0 / 128)
                        yield
                        RECIP(SS[p][:, 2:4], SS[p][:, 2:4], [K_("SSr")], [K_("SSr")])
                        STT("dve", QN[p][:, 0, :], INQK[p][:, 0, :], SS[p][:, 2:3], QG[:], ALU.mult, ALU.mult,
                            [K_("INQK"), K_("SSr"), "QG"], [K_("QN0")])
                        STT("dve", QN[p][:, 1, :], INQK[p][:, 1, :], SS[p][:, 3:4], KG[:], ALU.mult, ALU.mult,
                            [K_("INQK"), K_("SSr"), "KG"], [K_("QN1")])
                        pt = PT[p]
                        TR(pt[:, 0:128], QN[p][:, 0, :], IDB[:], [K_("QN0"), "IDB"], [K_("PT")])
                        TR(pt[:, 128:256], QN[p][:, 1, :], IDB[:], [K_("QN1"), "IDB"], [K_("PT")])
                        yield
                        l0 = t0 // dil
                        if dil == 1:
                            CP("act", QTA[:, t0:t0 + 128], pt[:, 0:128], [K_("PT")], ["QTA"])
                            CP("dve", KTP[:, 64 + t0:64 + t0 + 128], pt[:, 128:256], [K_("PT")], ["KTP"])
                        else:
                            CP("act", QTA[:, :].rearrange("d (r x) -> d r x", r=dil)[:, :, l0:l0 + lw],
                               pt[:, 0:128].rearrange("d (x r) -> d r x", r=dil), [K_("PT")], ["QTA"])
                            CP("dve", KTP[:, 0:dil * (L + 128)].rearrange("d (r x) -> d r x", r=dil)[:, :, 64 + l0:64 + l0 + lw],
                               pt[:, 128:256].rearrange("d (x r) -> d r x", r=dil), [K_("PT")], ["KTP"])

                run_streams([prep_lane(i) for i in range(NL)])

                blocks = [(r, b) for r in range(dil) for b in range(nblk)]

                def blk_lane(lane, g=g, dil=dil, L=L, nblk=nblk, nvt=nvt, s=s):
                    p = lane
                    K_ = lambda nm: "%s%d" % (nm, p)
                    for (r, b) in blocks[lane::NL]:
                        var = (1 if b == 0 else 0) + (2 if b == nblk - 1 else 0)
                        bo = ((g * 4 + s) * 4 + var) * 256
                        q0 = r * L + 128 * b
                        k0 = r * (L + 128) + 128 * b
                        pss = PS_[p]
                        MM(pss[:, 0:256], QTA[:, q0:q0 + 128], KTP[:, k0:k0 + 256], True, True, ["QTA", "KTP"], [K_("PS")])
                        yield
                        TT("dve", SB_[p][:], pss[:, 0:256], DB[:, bo:bo + 256], ALU.add, [K_("PS"), "DB"], [K_("SB")])
                        S.dve(_red_max(SS[p][:, 4:5], SB_[p][:]), [K_("SB")], [K_("SSm")])
                        TS("dve", SS[p][:, 5:6], SS[p][:, 4:5], -1.0, None, ALU.mult, None, [K_("SSm")], [K_("SSn")])
                        yield
                        ACT(PB_[p][:], SB_[p][:], AF.Exp, [K_("SB"), K_("SSn")], [K_("PB"), K_("SSd")],
                            bias=SS[p][:, 5:6], accum=SS[p][:, 6:7])
                        pt = PT[p]
                        TR(pt[:, 0:128], PB_[p][:, 0:128], IDB[:], [K_("PB"), "IDB"], [K_("PT")])
                        TR(pt[:, 128:256], PB_[p][:, 128:256], IDB[:], [K_("PB"), "IDB"], [K_("PT")])
                        yield
                        CP("act", PTS[p][:, 0, :], pt[:, 0:128], [K_("PT")], [K_("PTS0")])
                        CP("dve", PTS[p][:, 1, :], pt[:, 128:256], [K_("PT")], [K_("PTS1")])
                        RECIP(SS[p][:, 7:8], SS[p][:, 6:7], [K_("SSd")], [K_("SSi")])
                        MM(pss[:, 256:384], PTS[p][:, 0, :], VBA[:, r * nvt + b, :], True, False, [K_("PTS0"), "VBA_t%d" % (r * nvt + b)], [K_("PS")])
                        MM(pss[:, 256:384], PTS[p][:, 1, :], VBA[:, r * nvt + b + 1, :], False, True, [K_("PTS1"), "VBA_t%d" % (r * nvt + b + 1)], [K_("PS")])
                        yield
                        TS("dve", OS[p][:, 0:128], pss[:, 256:384], SS[p][:, 7:8], None, ALU.mult, None, [K_("PS"), K_("SSi")], ["OS%d" % p])
                        ACT(OS[p][:, 128:129], SS[p][:, 6:7], AF.Ln, [K_("SSd")], ["OS%d" % p])
                        TT("dve", OS[p][:, 128:129], OS[p][:, 128:129], SS[p][:, 4:5], ALU.add, ["OS%d" % p, K_("SSm")], ["OS%d" % p])
                        r0 = (128 * b) * dil + r
                        dst = og[g, s, r0:r0 + 127 * dil + 1:dil, :]
                        DMA("sp", dst, OS[p][:], ["OS%d" % p], ())

                run_streams([blk_lane(i) for i in range(NL)])
    S.barrier()
    with contextlib.ExitStack() as st:
        def sb(name, shape, dt):
            return st.enter_context(nc.sbuf_tensor("dc%d_%s" % (l, name), shape, dt))

        def ps(name, shape, dt):
            return st.enter_context(nc.psum_tensor("dcp%d_%s" % (l, name), shape, dt))

        IDB = sb("IDB", [128, 128], F32)
        YTS = sb("YTS", [128, T], BF16)
        IN3 = [sb("IN3_%d" % p, [128, 3, 256], F32) for p in range(2)]
        SS = [sb("SS%d" % p, [128, 12], F32) for p in range(2)]
        Y32 = [sb("Y32_%d" % p, [128, 128], F32) for p in range(2)]
        YB = [sb("YB%d" % p, [128, 128], F32) for p in range(2)]
        PT = [ps("PT%d" % p, [128, 512], F32) for p in range(2)]
        DMA("sp", IDB[:], ctx.c_ident, (), ["IDB"])
        cnt = dict(p=0)
        for s in range(4):
            for t in range(NT):
                p = cnt["p"] % 2
                cnt["p"] += 1
                K_ = lambda nm: "%s%d" % (nm, p)
                t0 = t * 128
                DMA(ctx.dq(), IN3[p][:], og[:, s, t0:t0 + 128, :].rearrange("g t c -> t g c"), (), [K_("IN3")])
                lse = IN3[p][:, :, 128]
                TT("dve", SS[p][:, 0:1], IN3[p][:, 0, 128:129], IN3[p][:, 1, 128:129], ALU.max, [K_("IN3")], [K_("SSm")])
                TT("dve", SS[p][:, 0:1], SS[p][:, 0:1], IN3[p][:, 2, 128:129], ALU.max, [K_("IN3"), K_("SSm")], [K_("SSm")])
                TS("dve", SS[p][:, 1:2], SS[p][:, 0:1], -1.0, None, ALU.mult, None, [K_("SSm")], [K_("SSn")])
                ACT(SS[p][:, 4:7], lse, AF.Exp, [K_("IN3"), K_("SSn")], [K_("SSe"), K_("SSd")],
                    bias=SS[p][:, 1:2], accum=SS[p][:, 2:3])
                RECIP(SS[p][:, 3:4], SS[p][:, 2:3], [K_("SSd")], [K_("SSi")])
                TS("dve", SS[p][:, 8:11], SS[p][:, 4:7], SS[p][:, 3:4], None, ALU.mult, None, [K_("SSe"), K_("SSi")], [K_("SSw")])
                TS("dve", Y32[p][:], IN3[p][:, 0, 0:128], SS[p][:, 8:9], None, ALU.mult, None, [K_("IN3"), K_("SSw")], [K_("Y32")])
                STT("dve", Y32[p][:], IN3[p][:, 1, 0:128], SS[p][:, 9:10], Y32[p][:], ALU.mult, ALU.add,
                    [K_("IN3"), K_("SSw"), K_("Y32")], [K_("Y32")])
                STT("dve", YB[p][:], IN3[p][:, 2, 0:128], SS[p][:, 10:11], Y32[p][:], ALU.mult, ALU.add,
                    [K_("IN3"), K_("SSw"), K_("Y32")], [K_("YB")])
                TR(PT[p][:, 0:128], YB[p][:], IDB[:], [K_("YB"), "IDB"], [K_("PT")])
                CP("act", YTS[:, t0:t0 + 128], PT[p][:, 0:128], [K_("PT")], ["YTS"])
            DMA("sp", yT[1536 + s * 128:1536 + (s + 1) * 128, :], YTS[:], ["YTS"], ())
    S.barrier()


def _red_max(out, in_):
    return lambda e: e.reduce_max(out=out, in_=in_, axis=AX.X)


def host_inputs(inp, b, T, DEPTH):
    f = lambda a: np.ascontiguousarray(np.asarray(a, dtype=np.float32))
    m = {}
    m["xT"] = f(np.asarray(inp["x"])[b, :T, :].T)
    for k in ("w_in", "w_out", "w_up", "w_down"):
        m[k] = f(np.asarray(inp[k])[:DEPTH])
    m["gmix"] = f(np.asarray(inp["norm_mix"])[:DEPTH].reshape(DEPTH, 16, 128).transpose(2, 0, 1).reshape(128, DEPTH * 16))
    m["gmlp"] = f(np.asarray(inp["norm_mlp"])[:DEPTH].reshape(DEPTH, 16, 128).transpose(2, 0, 1).reshape(128, DEPTH * 16))
    bc = lambda a: f(np.broadcast_to(np.asarray(a).reshape(1, -1), (128, np.asarray(a).size)))
    m["lbf_bc"] = bc(np.asarray(inp["hg_lb_fwd"]))
    m["lbb_bc"] = bc(np.asarray(inp["hg_lb_bwd"]))
    m["hgn_bc"] = bc(np.asarray(inp["hg_norm"])[:DEPTH])
    m["rtn_bc"] = bc(np.asarray(inp["ret_norm"])[:DEPTH])
    m["qn_bc"] = bc(np.asarray(inp["q_norm"])[:DEPTH])
    m["kn_bc"] = bc(np.asarray(inp["k_norm"])[:DEPTH])
    m["dbias"] = dil_bias_tables(np.asarray(inp["rel_bias"], dtype=np.float32))
    c = host_consts(T)
    for k in ("rope", "hgm", "ind", "ident", "rmask", "rpp", "rxi"):
        m[k] = f(c[k])
    return m


_CACHE = {}


def kernel(**inputs):
    T, DEPTH = 8192, 4
    if "nc" not in _CACHE:
        _CACHE["nc"] = build(T, DEPTH)[0]
    nc = _CACHE["nc"]
    maps = [host_inputs(inputs, c % 2, T, DEPTH) for c in range(2)]
    in_maps = [maps[c % 2] for c in range(8)]
    res = run_bass_kernel_spmd(nc, in_maps, core_ids=list(range(8)))
    out = np.stack([np.ascontiguousarray(res.results[b]["outT"].T) for b in range(2)], axis=0)
    return out.astype(np.float32)
```

```python
import contextlib
import math
import numpy as np
import concourse.bass as bass
import concourse.mybir as mybir
from concourse.bass_utils import run_bass_kernel_spmd

F32 = mybir.dt.float32
BF16 = mybir.dt.bfloat16
ALU = mybir.AluOpType
AF = mybir.ActivationFunctionType
AX = mybir.AxisListType

D = 2048
INW = 10752
DFF = 8192
EPS = 1e-6
NEG = -1.0e30
DBG = {}
HG_Q, HG_I, HG_ZF, HG_ZB, HG_G = 0, 768, 1536, 2304, 3072
RT_Q, RT_K, RT_V, RT_G = 3840, 4224, 4608, 5376
DL0 = 6144


class Sched:
    ENGS = ("pe", "dve", "act", "pool", "sp")
    NDMA = 16

    def __init__(self, nc, same_engine_sync=True):
        self.nc = nc
        self.ops = []
        self.last_w = {}
        self.readers = {}
        self.same = same_engine_sync
        self.bar = set()
        self.since = set()
        self.lastop = {}

    EXCL = ("PA", "PO", "PKV", "PT", "PS", "PB")

    def op(self, eng, fn, reads=(), writes=(), dma=False):
        ex = [k for k in reads if k.startswith(self.EXCL)]
        if ex:
            reads = [k for k in reads if k not in ex]
            writes = list(writes) + [k for k in ex if k not in writes]
        deps = set(self.bar)
        for k in reads:
            if k in self.last_w:
                deps.add(self.last_w[k])
        for k in writes:
            if k in self.last_w:
                deps.add(self.last_w[k])
            deps.update(self.readers.get(k, ()))
        i = len(self.ops)
        self.ops.append(dict(eng=eng, fn=fn, deps=deps, dma=dma))
        for k in reads:
            self.readers.setdefault(k, []).append(i)
        for k in writes:
            self.last_w[k] = i
            self.readers[k] = []
        if dma:
            self.since.add(i)
        else:
            self.lastop[eng] = i
        return i

    def barrier(self):
        self.bar = set(self.lastop.values()) | self.since
        self.since = set()
        self.last_w = {}
        self.readers = {}

    def pe(self, fn, r=(), w=()): return self.op("pe", fn, r, w)
    def dve(self, fn, r=(), w=()): return self.op("dve", fn, r, w)
    def act(self, fn, r=(), w=()): return self.op("act", fn, r, w)
    def pool(self, fn, r=(), w=()): return self.op("pool", fn, r, w)
    def dma(self, q, fn, r=(), w=()): return self.op(q, fn, r, w, dma=True)

    def emit(self, final_wait_ops=()):
        nc = self.nc
        ops = self.ops
        n = len(ops)
        needed = [False] * n
        for o in ops:
            for d in o["deps"]:
                od = ops[d]
                if (not od["dma"]) and (not o["dma"]) and od["eng"] == o["eng"] and (o["eng"] == "pe" or not self.same):
                    continue
                needed[d] = True
        for d in final_wait_ops:
            needed[d] = True
        with contextlib.ExitStack() as st:
            esem = {e: st.enter_context(nc.semaphore("s_" + e)) for e in self.ENGS}
            dsem = [st.enter_context(nc.semaphore("d_%d" % i)) for i in range(self.NDMA)]
            ecnt = {e: 0 for e in self.ENGS}
            dcnt = [0] * self.NDMA
            rr = {"pool": 0, "hw": 0}
            half = self.NDMA // 2
            tok = [None] * n
            prevdma = [None] * n
            for i, o in enumerate(ops):
                if o["dma"]:
                    if o["eng"] == "pool":
                        s = rr["pool"] % half
                        rr["pool"] += 1
                    else:
                        s = half + rr["hw"] % half
                        rr["hw"] += 1
                    if dcnt[s] > 0:
                        prevdma[i] = (s, 16 * dcnt[s], None)
                    dcnt[s] += 1
                    tok[i] = (s, 16 * dcnt[s], None)
                elif needed[i]:
                    ecnt[o["eng"]] += 1
                    tok[i] = (o["eng"], ecnt[o["eng"]], o["eng"])
            self.stats = dict(n=n, ecnt=dict(ecnt), dcnt=list(dcnt))
            block = st.enter_context(nc.Block())

            def semof(k):
                return dsem[k] if isinstance(k, int) else esem[k]

            def run(ename):
                def body(eng):
                    waited = {}
                    for i, o in enumerate(ops):
                        if o["eng"] != ename:
                            continue
                        ws = []
                        for d in o["deps"]:
                            t = tok[d]
                            if t is None:
                                continue
                            if t[2] == ename and (ename == "pe" or not self.same):
                                continue
                            ws.append(t)
                        if prevdma[i] is not None:
                            ws.append(prevdma[i])
                        ws.sort(key=lambda t: -t[1])
                        for (sk, val, _) in ws:
                            if waited.get(sk, 0) >= val:
                                continue
                            waited[sk] = val
                            eng.wait_ge(semof(sk), val)
                        ins = o["fn"](eng)
                        if o["dma"]:
                            ins.then_inc(semof(tok[i][0]), 16)
                        elif needed[i]:
                            ins.then_inc(semof(tok[i][0]), 1)
                    if ename == "sp":
                        for d in final_wait_ops:
                            sk, val, _ = tok[d]
                            eng.wait_ge(semof(sk), val)
                return body

            block.tensor(run("pe"))
            block.vector(run("dve"))
            block.scalar(run("act"))
            block.gpsimd(run("pool"))
            block.sync(run("sp"))


def host_consts(T):
    c = {}
    NT = T // 128
    half = 32
    inv = (10000.0 ** (-np.arange(half, dtype=np.float32) / half)).astype(np.float32)
    ang = np.arange(T, dtype=np.float32)[:, None] * inv[None, :]
    cos = np.cos(ang).astype(np.float32).reshape(NT, 128, half).transpose(1, 0, 2)
    sin = np.sin(ang).astype(np.float32).reshape(NT, 128, half).transpose(1, 0, 2)
    c["rope"] = np.ascontiguousarray(np.concatenate([cos, cos, -sin, sin], axis=2))
    j = np.arange(128)[:, None]
    i = np.arange(128)[None, :]
    same = (j // 32) == (i // 32)
    hgm = np.zeros((128, 6, 128), np.float32)
    hgm[:, 0] = same & (j <= i)
    hgm[:, 1] = same & (j > i)
    hgm[:, 2] = same & (j >= i)
    hgm[:, 3] = same & (j < i)
    hgm[:, 4] = same & (j <= i)
    hgm[:, 5] = same & (j >= i)
    c["hgm"] = hgm
    ind = np.zeros((128, 4), np.float32)
    ind[np.arange(128), np.arange(128) // 32] = 1.0
    c["ind"] = ind
    c["ident"] = np.eye(128, dtype=np.float32)
    hidx = np.arange(6, dtype=np.float64)
    lgf = np.log1p(-np.exp2(-5.0 - hidx))
    lgb = lgf[::-1]
    idx = np.arange(128, dtype=np.float64)
    rel = idx[None, :] - idx[:, None]
    rmask = np.zeros((128, 6, 128), np.float32)
    rpp = np.zeros((128, 6, 4), np.float32)
    rxi = np.zeros((64, 6, 2, 128), np.float32)
    rcd = np.zeros((6, 2), np.float64)
    for hh in range(6):
        m = np.where(rel > 0, np.exp(lgf[hh] * np.maximum(rel, 0)), 0.0) + \
            np.where(rel < 0, np.exp(lgb[hh] * np.maximum(-rel, 0)), 0.0) + np.where(rel == 0, 2.0, 0.0)
        rmask[:, hh, :] = m
        rpp[:, hh, 0] = np.exp(lgf[hh] * (127 - idx))
        rpp[:, hh, 1] = np.exp(lgb[hh] * idx)
        rxi[:, hh, 0, :] = np.exp(lgf[hh] * (idx + 1))[None]
        rxi[:, hh, 1, :] = np.exp(lgb[hh] * (128 - idx))[None]
        rcd[hh, 0] = np.exp(lgf[hh] * 128)
        rcd[hh, 1] = np.exp(lgb[hh] * 128)
    c["rmask"] = rmask * np.float32(0.125)
    c["rpp"] = rpp
    c["rxi"] = (rxi * np.float32(0.125)).reshape(64, 6 * 2 * 128)
    c["_rcd"] = rcd
    return c


def t5_bucket_np(rel):
    nb = 16
    max_exact = 8
    sign_off = np.where(rel > 0, nb, 0)
    n = np.abs(rel)
    nf = np.maximum(n, 1).astype(np.float32)
    large = max_exact + (np.log(nf / np.float32(max_exact)) / np.float32(math.log(1024 / max_exact))
                         * np.float32(nb - max_exact)).astype(np.int32)
    large = np.minimum(large, nb - 1)
    return sign_off + np.where(n < max_exact, n, large)


def dil_bias_tables(rel_bias):
    out = np.full((128, 3, 4, 4, 256), NEG, np.float32)
    q = np.arange(128)[:, None]
    k = np.arange(256)[None, :]
    rel = k - 64 - q
    band = np.abs(rel) <= 64
    for g, dil in enumerate((1, 4, 16)):
        bk = t5_bucket_np(rel * dil)
        for s in range(4):
            vals = rel_bias[:, g * 4 + s][bk]
            for v in range(4):
                ok = band.copy()
                if v & 1:
                    ok &= (k >= 64)
                if v & 2:
                    ok &= (k < 192)
                out[:, g, s, v, :] = np.where(ok, vals, np.float32(NEG))
    return out.reshape(128, 3 * 4 * 4 * 256)


def build(T, DEPTH, debug=False, stop_after=None, mixers=("hg", "ret", "dil")):
    nc = bass.Bass("TRN2", target_bir_lowering=False)
    NT = T // 128
    NSB = T // 1024
    kind_dbg = "ExternalOutput" if debug else None

    def din(name, shape, dt=F32):
        return nc.dram_tensor(name, list(shape), dt, kind="ExternalInput").ap()

    def dscr(name, shape, dt=F32, out=False):
        if out or debug:
            return nc.dram_tensor(name, list(shape), dt, kind="ExternalOutput").ap()
        return nc.dram_tensor(name, list(shape), dt).ap()

    xT_in = din("xT", [D, T])
    w_in = din("w_in", [DEPTH, D, INW])
    w_out = din("w_out", [DEPTH, D, D])
    w_up = din("w_up", [DEPTH, D, DFF])
    w_down = din("w_down", [DEPTH, DFF, D])
    gmix = din("gmix", [128, DEPTH * 16])
    gmlp = din("gmlp", [128, DEPTH * 16])
    lbf_bc = din("lbf_bc", [128, 4 * 768])
    lbb_bc = din("lbb_bc", [128, 4 * 768])
    hgn_bc = din("hgn_bc", [128, DEPTH * 768])
    rtn_bc = din("rtn_bc", [128, DEPTH * 768])
    qn_bc = din("qn_bc", [128, DEPTH * 128])
    kn_bc = din("kn_bc", [128, DEPTH * 128])
    dbias = din("dbias", [128, 3 * 4 * 4 * 256])
    c_rope = din("rope", [128, NT, 128])
    c_hgm = din("hgm", [128, 6, 128])
    c_ind = din("ind", [128, 4])
    c_ident = din("ident", [128, 128])
    c_rmask = din("rmask", [128, 6, 128])
    c_rpp = din("rpp", [128, 6, 4])
    c_rxi = din("rxi", [64, 6 * 2 * 128])
    outT = dscr("outT", [D, T], out=True)
    NH = DBG.get("NH", 2 if T * INW * 4 > 200e6 else 1)
    HR = T // NH
    hparts = [dscr("h_scr%d" % i, [HR, INW]) for i in range(NH)]

    def hrows(t0, n=128):
        i = t0 // HR
        return hparts[i][t0 - i * HR:t0 - i * HR + n, :]
    yT = dscr("yT_scr", [D, T], BF16)
    og = dscr("og_scr", [3, 4, T, 256])
    rcd = host_consts(128)["_rcd"]

    S = Sched(nc, same_engine_sync=DBG.get('same', True))
    qrr = [0]

    def dq():
        qrr[0] += 1
        return "sp"

    def MM(out, lhsT, rhs, start, stop, r, w):
        return S.pe(lambda e: e.matmul(out, lhsT=lhsT, rhs=rhs, start=start, stop=stop), r, w)

    def TR(out, in_, ident, r, w):
        return S.pe(lambda e: e.transpose(out, in_, ident), r, w)

    def ACT(out, in_, func, r, w, bias=None, scale=None, accum=None):
        kw = {}
        if bias is not None:
            kw["bias"] = bias
        if scale is not None:
            kw["scale"] = scale
        if accum is not None:
            kw["accum_out"] = accum
        return S.act(lambda e: e.activation(out=out, in_=in_, func=func, **kw), r, w)

    def TT(eng, out, in0, in1, op, r, w):
        return S.op(eng, lambda e: e.tensor_tensor(out=out, in0=in0, in1=in1, op=op), r, w)

    def TS(eng, out, in0, s1, s2, op0, op1, r, w):
        if op1 is None:
            return S.op(eng, lambda e: e.tensor_scalar(out=out, in0=in0, scalar1=s1, scalar2=None, op0=op0), r, w)
        return S.op(eng, lambda e: e.tensor_scalar(out=out, in0=in0, scalar1=s1, scalar2=s2, op0=op0, op1=op1), r, w)

    def RECIP(out, in_, r, w):
        return S.dve(lambda e: e.reciprocal(out=out, in_=in_), r, w)

    def STT(eng, out, in0, scalar, in1, op0, op1, r, w):
        return S.op(eng, lambda e: e.scalar_tensor_tensor(out=out, in0=in0, scalar=scalar, in1=in1, op0=op0, op1=op1), r, w)

    def CP(eng, out, in_, r, w):
        if eng == "act":
            return S.act(lambda e: e.copy(out=out, in_=in_), r, w)
        return S.op(eng, lambda e: e.tensor_copy(out=out, in_=in_), r, w)

    def DMA(q, out, in_, r, w):
        return S.dma(q, lambda e: e.dma_start(out=out, in_=in_), r, w)

    def MEMSET(eng, ap, val, w):
        return S.op(eng, lambda e: e.memset(ap, val), (), w)

    final_ops = []

    def dense_stage(l):
        with contextlib.ExitStack() as st:
            def sb(name, shape, dt):
                return st.enter_context(nc.sbuf_tensor("ds%d_%s" % (l, name), shape, dt))

            def ps(name, shape, dt):
                return st.enter_context(nc.psum_tensor("dsp%d_%s" % (l, name), shape, dt))

            XT = sb("XT", [128, 16, 1024], F32)
            AB = sb("AB", [128, 16, 1024], BF16)
            WB = [sb("WB%d" % i, [128, 16, 512], BF16) for i in range(2)]
            WD = [sb("WD%d" % i, [128, 4, 2048], BF16) for i in range(2)]
            AT = sb("AT", [128, 4, 1024], BF16)
            RS = sb("RS", [128, 1024], F32)
            SQ = [sb("SQ%d" % i, [128, 512], F32) for i in range(2)]
            RL = [sb("RL%d" % i, [128, 512], F32) for i in range(2)]
            HS = [sb("HS%d" % i, [128, 512], F32) for i in range(3)]
            ONES = sb("ONES", [128, 128], F32)
            G1 = sb("G1", [128, DEPTH * 16], F32)
            G2 = sb("G2", [128, DEPTH * 16], F32)
            PB = [ps("PB%d" % i, [128, 512], F32) for i in range(6)]
            PSTAT = [ps("PS%d" % i, [128, 512], F32) for i in range(2)]
            MEMSET("pool", ONES[:], 1.0, ["ONES"])
            DMA("sp", G1[:], gmix, (), ["G1"])
            DMA("sp", G2[:], gmlp, (), ["G2"])
            cnt = dict(pb=0, wb=0, wd=0, sq=0, rl=0, hs=0, ev=0)

            def next_pb():
                cnt["pb"] += 1
                return cnt["pb"] % 6

            def load_wb(src):
                i = cnt["wb"] % 2
                cnt["wb"] += 1
                v = src.rearrange("(c p) n -> p c n", p=128)
                for q4 in range(4):
                    DMA("pool", WB[i][:, q4 * 4:(q4 + 1) * 4, :], v[:, q4 * 4:(q4 + 1) * 4, :], (), ["WB%d_%d" % (i, q4)])
                return i

            def load_wd(src):
                i = cnt["wd"] % 2
                cnt["wd"] += 1
                v = src.rearrange("(c p) n -> p c n", p=128)
                for q4 in range(4):
                    DMA("pool", WD[i][:, q4:q4 + 1, :], v[:, q4:q4 + 1, :], (), ["WD%d_%d" % (i, q4)])
                return i

            def norm_to_AB(gtile, goff):
                for th in range(2):
                    pst = PSTAT[th]
                    for c in range(16):
                        i = cnt["sq"] % 2
                        cnt["sq"] += 1
                        ACT(SQ[i][:], XT[:, c, th * 512:(th + 1) * 512], AF.Square, ["XT%d" % c], ["SQ%d" % i])
                        MM(pst[:], ONES[:], SQ[i][:], c == 0, c == 15, ["ONES", "SQ%d" % i], ["PST%d" % th])
                    sl = RS[:, th * 512:(th + 1) * 512]
                    ACT(sl, pst[:], AF.Sqrt, ["PST%d" % th], ["RS%d" % th], bias=EPS, scale=1.0 / D)
                    RECIP(sl, sl, ["RS%d" % th], ["RS%d" % th])
                for c in range(16):
                    STT("dve", AB[:, c, :], XT[:, c, :], gtile[:, goff + c:goff + c + 1], RS[:], ALU.mult, ALU.mult,
                        ["XT%d" % c, "RS0", "RS1", "G1", "G2"], ["AB%d" % c])

            for sbi in range(NSB):
                t0 = sbi * 1024
                src = xT_in if l <= 1 else outT
                for c in range(16):
                    DMA(dq(), XT[:, c, :], src[c * 128:(c + 1) * 128, t0:t0 + 1024], (), ["XT%d" % c])
                if l > 0:
                    lp = l - 1
                    for c in range(16):
                        DMA(dq(), AB[:, c, :], yT[c * 128:(c + 1) * 128, t0:t0 + 1024], (), ["AB%d" % c])
                    abk = ["AB%d" % c for c in range(16)]
                    for dg in range(4):
                        wi = load_wb(w_out[lp, :, dg * 512:(dg + 1) * 512])
                        for dc in range(4):
                            dch = dg * 4 + dc
                            for th in range(2):
                                pb = next_pb()
                                for k in range(16):
                                    MM(PB[pb][:], WB[wi][:, k, dc * 128:(dc + 1) * 128], AB[:, k, th * 512:(th + 1) * 512],
                                       k == 0, k == 15, ["WB%d_%d" % (wi, k // 4), "AB%d" % k], ["PB%d" % pb])
                                xs = XT[:, dch, th * 512:(th + 1) * 512]
                                TT("dve", xs, xs, PB[pb][:], ALU.add, ["PB%d" % pb, "XT%d" % dch], ["XT%d" % dch])
                    norm_to_AB(G2, lp * 16)
                    for fg in range(16):
                        wi = load_wb(w_up[lp, :, fg * 512:(fg + 1) * 512])
                        wdi = load_wd(w_down[lp, fg * 512:(fg + 1) * 512, :])
                        for fc in range(4):
                            for th in range(2):
                                pb = next_pb()
                                for k in range(16):
                                    MM(PB[pb][:], WB[wi][:, k, fc * 128:(fc + 1) * 128], AB[:, k, th * 512:(th + 1) * 512],
                                       k == 0, k == 15, ["WB%d_%d" % (wi, k // 4), "AB%d" % k], ["PB%d" % pb])
                                i = cnt["rl"] % 2
                                cnt["rl"] += 1
                                ACT(RL[i][:], PB[pb][:], AF.Relu, ["PB%d" % pb], ["RL%d" % i])
                                ACT(AT[:, fc, th * 512:(th + 1) * 512], RL[i][:], AF.Square, ["RL%d" % i], ["AT%d_%d" % (fc, th)])
                        for dch in range(16):
                            for th in range(2):
                                pb = next_pb()
                                for fc in range(4):
                                    MM(PB[pb][:], WD[wdi][:, fc, dch * 128:(dch + 1) * 128], AT[:, fc, th * 512:(th + 1) * 512],
                                       fc == 0, fc == 3, ["WD%d_%d" % (wdi, fc), "AT%d_%d" % (fc, th)], ["PB%d" % pb])
                                xs = XT[:, dch, th * 512:(th + 1) * 512]
                                TT("dve", xs, xs, PB[pb][:], ALU.add, ["PB%d" % pb, "XT%d" % dch], ["XT%d" % dch])
                    for c in range(16):
                        o = DMA(dq(), outT[c * 128:(c + 1) * 128, t0:t0 + 1024], XT[:, c, :], ["XT%d" % c], ())
                        if l == DEPTH:
                            final_ops.append(o)
                if l < DEPTH:
                    norm_to_AB(G1, l * 16)
                    for cg in range(21):
                        wi = load_wb(w_in[l, :, cg * 512:(cg + 1) * 512])
                        for tt in range(8):
                            pb = next_pb()
                            for k in range(16):
                                MM(PB[pb][:], AB[:, k, tt * 128:(tt + 1) * 128], WB[wi][:, k, :],
                                   k == 0, k == 15, ["WB%d_%d" % (wi, k // 4), "AB%d" % k], ["PB%d" % pb])
                            i = cnt["hs"] % 3
                            cnt["hs"] += 1
                            cnt["ev"] += 1
                            CP("act" if cnt["ev"] % 2 else "dve", HS[i][:], PB[pb][:], ["PB%d" % pb], ["HS%d" % i])
                            DMA(dq(), hrows(t0 + tt * 128)[:, cg * 512:(cg + 1) * 512], HS[i][:], ["HS%d" % i], ())
        S.barrier()

    from types import SimpleNamespace
    ctx = SimpleNamespace(**locals())
    for l in range(DEPTH + 1):
        dense_stage(l)
        if stop_after == "dense0":
            break
        if l < DEPTH:
            if "hg" in mixers:
                mixer_hg(ctx, l)
            if "ret" in mixers:
                mixer_ret(ctx, l)
            if "dil" in mixers:
                mixer_dil(ctx, l)
            if stop_after == "mix0":
                break
    if not final_ops:
        final_ops.append(S.dma("sp", lambda e: e.dma_start(out=outT[0:128, 0:128], in_=xT_in[0:128, 0:128]), (), ()))
    S.emit(final_wait_ops=final_ops)
    return nc, S


def run_streams(gens):
    gens = list(gens)
    while gens:
        for g in list(gens):
            try:
                next(g)
            except StopIteration:
                gens.remove(g)


def mixer_hg(ctx, l):
    nc, S, T, NT = ctx.nc, ctx.S, ctx.T, ctx.NT
    MM, TR, ACT, TT, TS, STT, CP, DMA, MEMSET, RECIP = (ctx.MM, ctx.TR, ctx.ACT, ctx.TT, ctx.TS, ctx.STT, ctx.CP,
                                                        ctx.DMA, ctx.MEMSET, ctx.RECIP)
    yT = ctx.yT
    with contextlib.ExitStack() as st0:
        OML = [st0.enter_context(nc.sbuf_tensor("hg%d_OML%d" % (l, d), [128, 768], F32)) for d in range(2)]
        with contextlib.ExitStack() as st:
            def sb(name, shape, dt):
                return st.enter_context(nc.sbuf_tensor("hgl%d_%s" % (l, name), shape, dt))
            LBR = sb("LBR", [128, 4 * 768], F32)
            LBT = sb("LBT", [128, 768], F32)
            LBX = sb("LBX", [128, 768], F32)
            for d, src in enumerate((ctx.lbf_bc, ctx.lbb_bc)):
                if l == 0:
                    MEMSET("dve", OML[d][:], 1.0, ["OML%d" % d])
                    continue
                DMA("sp", LBR[:], src, (), ["LBR"])
                L4 = [LBR[:, i * 768:(i + 1) * 768] for i in range(4)]
                TT("dve", LBT[:], L4[0], L4[1], ALU.max, ["LBR"], ["LBT"])
                TT("dve", LBT[:], LBT[:], L4[2], ALU.max, ["LBR", "LBT"], ["LBT"])
                TT("dve", LBT[:], LBT[:], L4[3], ALU.max, ["LBR", "LBT"], ["LBT"])
                for i in range(4):
                    TT("dve", L4[i], L4[i], LBT[:], ALU.subtract, ["LBR", "LBT"], ["LBR"])
                ACT(LBR[:], LBR[:], AF.Exp, ["LBR"], ["LBR"])
                TT("dve", LBT[:], L4[0], L4[1], ALU.add, ["LBR"], ["LBT"])
                TT("dve", LBT[:], LBT[:], L4[2], ALU.add, ["LBR", "LBT"], ["LBT"])
                TT("dve", LBT[:], LBT[:], L4[3], ALU.add, ["LBR", "LBT"], ["LBT"])
                RECIP(LBT[:], LBT[:], ["LBT"], ["LBT"])
                CP("dve", LBX[:], L4[1], ["LBR"], ["LBX"])
                for i in range(2, l + 1):
                    TT("dve", LBX[:], LBX[:], L4[i], ALU.add, ["LBR", "LBX"], ["LBX"])
                TT("dve", LBX[:], LBX[:], LBT[:], ALU.mult, ["LBT", "LBX"], ["LBX"])
                TS("dve", OML[d][:], LBX[:], -1.0, 1.0, ALU.mult, ALU.add, ["LBX"], ["OML%d" % d])
        S.barrier()
        with contextlib.ExitStack() as st:
            def sb(name, shape, dt):
                return st.enter_context(nc.sbuf_tensor("hg%d_%s" % (l, name), shape, dt))

            def ps(name, shape, dt):
                return st.enter_context(nc.psum_tensor("hgp%d_%s" % (l, name), shape, dt))

            HGM = sb("HGM", [128, 6, 128], F32)
            IND = sb("IND", [128, 4], F32)
            IDB = sb("IDB", [128, 128], F32)
            HGN = sb("HGN", [128, 768], F32)
            DMA("sp", HGM[:], ctx.c_hgm, (), ["HGM"])
            DMA("sp", IND[:], ctx.c_ind, (), ["IND"])
            DMA("sp", IDB[:], ctx.c_ident, (), ["IDB"])
            DMA("sp", HGN[:], ctx.hgn_bc[:, l * 768:(l + 1) * 768], (), ["HGN"])
            NSTR = 3
            NRING = 8
            PZ = [ps("PZ_%d" % z, [128, 512], F32) for z in range(2)]
            ST = []
            for si in range(NSTR):
                X = dict()
                X["OF"] = sb("OF_%d" % si, [128, NT, 128], F32)
                X["YS"] = [sb("YS%d_%d" % (p, si), [128, 256], BF16) for p in range(2)]
                X["IN"] = [sb("IN%d_%d" % (p, si), [128, 5, 128], F32) for p in range(2)]
                for nm in ("SIGN", "LOGF", "KK", "EB", "ENB", "EBR", "KH", "O", "GSG", "QT", "KT", "Y"):
                    X[nm] = [sb("%s_%d" % (nm, si), [128, 128], F32)]
                for nm in ("VB", "QTT", "KTT", "AM"):
                    X[nm] = [sb("%s_%d" % (nm, si), [128, 128], BF16)]
                X["KHZ"] = [sb("KHZ_%d" % si, [128, 4, 128], BF16)]
                X["QZ"] = [sb("QZ_%d" % si, [128, 640], F32)]
                X["FD"] = [sb("FD_%d" % si, [128, 4], F32)]
                X["SS"] = [sb("SS_%d" % si, [128, 2], F32)]
                X["S32"] = [sb("S32_%d_%d" % (p, si), [128, 128], F32) for p in range(NRING)]
                X["PA"] = ps("PA_%d" % si, [128, 512], F32)
                X["PKV"] = ps("PKV_%d" % si, [128, 4, 128], F32)
                X["PT"] = PZ[si % 2]
                MEMSET("pool", X["QZ"][0][:], 0.0, ["QZ0_%d" % si])
                ST.append(X)

            def stream(si, hd):
                X = ST[si]
                hs = slice(hd * 128, (hd + 1) * 128)
                cnt = dict(p=0, r=0)
                kb = lambda nm: ("PT_%d" % (si % 2)) if nm == "PT" else ("PA_%d" % si if nm in ("PA", "PO") else "%s_%d" % (nm, si))
                pa, pkv, pt = X["PA"], X["PKV"], X["PT"]
                po = pa
                fo = 384 + 4 * (si // 2)
                for d in range(2):
                    order = list(range(NT)) if d == 0 else list(range(NT - 1, -1, -1))
                    corder = [0, 1, 2, 3] if d == 0 else [3, 2, 1, 0]
                    cur = cnt["r"] % NRING
                    cnt["r"] += 1
                    MEMSET("dve", X["S32"][cur][:], 0.0, ["S32_%d_%d" % (cur, si)])
                    for t in order:
                        pi = cnt["p"] % 2
                        cnt["p"] += 1
                        p = 0
                        P = lambda nm: X[nm][0][:]
                        K_ = lambda nm, pi=pi: ("IN%d_%d" % (pi, si)) if nm == "IN" else "%s0_%d" % (nm, si)
                        IN = X["IN"][pi]
                        t0 = t * 128
                        src = ctx.hrows(t0)[:, 0:3840].rearrange("t (s c) -> t s c", s=5)[:, :, hs]
                        DMA("sp", IN[:], src, (), [K_("IN")])
                        ACT(P("SIGN"), IN[:, 2 + d, :], AF.Sigmoid, [K_("IN")], [K_("SIGN")], scale=-1.0)
                        CP("pool", P("VB"), IN[:, 1, :], [K_("IN")], [K_("VB")])
                        yield
                        TT("dve", P("KK"), P("SIGN"), OML[d][:, hs], ALU.mult, [K_("SIGN"), "OML%d" % d], [K_("KK")])
                        ACT(P("LOGF"), P("KK"), AF.Ln, [K_("KK")], [K_("LOGF")], scale=-1.0, bias=1.0)
                        MM(pa[:, 0:128], HGM[:, 2 * d, :], P("LOGF"), True, True, ["HGM", K_("LOGF")], [kb("PA")])
                        MM(pa[:, 128:256], HGM[:, 2 * d + 1, :], P("LOGF"), True, True, ["HGM", K_("LOGF")], [kb("PA")])
                        MM(pt[:, fo:fo + 4], P("LOGF"), IND[:], True, True, ["IND", K_("LOGF")], [kb("PT")])
                        yield
                        ACT(P("EB"), pa[:, 0:128], AF.Exp, [kb("PA")], [K_("EB")])
                        ACT(P("ENB"), pa[:, 0:128], AF.Exp, [kb("PA")], [K_("ENB")], scale=-1.0)
                        ACT(P("EBR"), pa[:, 128:256], AF.Exp, [kb("PA")], [K_("EBR")])
                        ACT(X["FD"][p][:], pt[:, fo:fo + 4], AF.Exp, [kb("PT")], [K_("FD")])
                        TT("dve", P("QT"), IN[:, 0, :], P("EB"), ALU.mult, [K_("IN"), K_("EB")], [K_("QT")])
                        TT("dve", P("KT"), P("KK"), P("ENB"), ALU.mult, [K_("KK"), K_("ENB")], [K_("KT")])
                        TT("dve", P("KH"), P("KK"), P("EBR"), ALU.mult, [K_("KK"), K_("EBR")], [K_("KH")])
                        for c in range(4):
                            TS("dve" if c < 2 else "pool", X["KHZ"][p][:, c, :], P("KH"), IND[:, c:c + 1], None, ALU.mult, None,
                               [K_("KH"), "IND"], [K_("KHZ")])
                        yield
                        TR(pt[:, 0:128], P("QT"), IDB[:], [K_("QT"), "IDB"], [kb("PT")])
                        TR(pt[:, 128:256], P("KT"), IDB[:], [K_("KT"), "IDB"], [kb("PT")])
                        CP("act", P("QTT"), pt[:, 0:128], [kb("PT")], [K_("QTT")])
                        CP("act", P("KTT"), pt[:, 128:256], [kb("PT")], [K_("KTT")])
                        CP("dve", X["QZ"][p][:].rearrange("p (c x) -> p c x", x=160)[:, :, 0:32],
                           pt[:, 0:128].rearrange("p (c x) -> p c x", x=32), [kb("PT")], [K_("QZ")])
                        for c in range(4):
                            MM(pkv[:, c, :], X["KHZ"][p][:, c, :], P("VB"), True, True, [K_("KHZ"), K_("VB")], [kb("PKV")])
                        yield
                        MM(pa[:, 256:384], P("KTT"), P("QTT"), True, True, [K_("KTT"), K_("QTT")], [kb("PA")])
                        TT("dve", P("AM"), pa[:, 256:384], HGM[:, 4 + d, :], ALU.mult, [kb("PA"), "HGM"], [K_("AM")])
                        states = [cur]
                        for c in corder:
                            nxt = cnt["r"] % NRING
                            cnt["r"] += 1
                            STT("dve", X["S32"][nxt][:], X["S32"][cur][:], X["FD"][p][:, c:c + 1], pkv[:, c, :], ALU.mult, ALU.add,
                                ["S32_%d_%d" % (cur, si), K_("FD"), kb("PKV")], ["S32_%d_%d" % (nxt, si)])
                            cur = nxt
                            states.append(cur)
                        yield
                        MM(po[:, 384:512], P("AM"), P("VB"), True, False, [K_("AM"), K_("VB")], [kb("PO")])
                        for ci, c in enumerate(corder):
                            sc = states[ci]
                            MM(po[:, 384:512], X["QZ"][p][:, 128 * c:128 * c + 128], X["S32"][sc][:], False, ci == 3,
                               [K_("QZ"), "S32_%d_%d" % (sc, si)], [kb("PO")])
                        if d == 0:
                            CP("act", X["OF"][:, t, :], po[:, 384:512], [kb("PO")], ["OF%d_%d" % (t, si)])
                        else:
                            SSp = X["SS"][p]
                            TT("dve", P("O"), X["OF"][:, t, :], po[:, 384:512], ALU.add, ["OF%d_%d" % (t, si), kb("PO")], [K_("O")])
                            ACT(P("KH"), P("O"), AF.Square, [K_("O")], [K_("KH"), K_("SS")], accum=SSp[:, 0:1])
                            ACT(SSp[:, 1:2], SSp[:, 0:1], AF.Sqrt, [K_("SS")], [K_("SS")], bias=EPS, scale=1.0 / 128)
                            RECIP(SSp[:, 1:2], SSp[:, 1:2], [K_("SS")], [K_("SS")])
                            ACT(P("GSG"), IN[:, 4, :], AF.Silu, [K_("IN")], [K_("GSG")])
                            TT("pool", P("GSG"), P("GSG"), HGN[:, hs], ALU.mult, [K_("GSG"), "HGN"], [K_("GSG")])
                            STT("dve", P("Y"), P("O"), SSp[:, 1:2], P("GSG"), ALU.mult, ALU.mult,
                                [K_("O"), K_("SS"), K_("GSG")], [K_("Y")])
                            TR(pt[:, 256:384], P("Y"), IDB[:], [K_("Y"), "IDB"], [kb("PT")])
                            yp = (t // 2) % 2
                            CP("act", X["YS"][yp][:, (t % 2) * 128:(t % 2) * 128 + 128], pt[:, 256:384], [kb("PT")], ["YS%d_%d" % (yp, si)])
                            if t % 2 == 0:
                                DMA("sp", yT[hd * 128:(hd + 1) * 128, t0:t0 + 256], X["YS"][yp][:], ["YS%d_%d" % (yp, si)], ())
                        yield

            for h0 in range(0, 6, NSTR):
                run_streams([stream(si, h0 + si) for si in range(NSTR)])
    S.barrier()


def mixer_ret(ctx, l):
    nc, S, T, NT = ctx.nc, ctx.S, ctx.T, ctx.NT
    MM, TR, ACT, TT, TS, STT, CP, DMA, MEMSET, RECIP = (ctx.MM, ctx.TR, ctx.ACT, ctx.TT, ctx.TS, ctx.STT, ctx.CP,
                                                        ctx.DMA, ctx.MEMSET, ctx.RECIP)
    yT, rcd = ctx.yT, ctx.rcd
    with contextlib.ExitStack() as st:
        def sb(name, shape, dt):
            return st.enter_context(nc.sbuf_tensor("rt%d_%s" % (l, name), shape, dt))

        def ps(name, shape, dt):
            return st.enter_context(nc.psum_tensor("rtp%d_%s" % (l, name), shape, dt))

        RMASK = sb("RMASK", [128, 6, 128], F32)
        RPP = sb("RPP", [128, 6, 4], F32)
        RXI = sb("RXI", [64, 6 * 2 * 128], F32)
        ROPE = sb("ROPE", [128, NT, 128], F32)
        IDB = sb("IDB", [128, 128], F32)
        RTN = sb("RTN", [128, 768], F32)
        OF = sb("OF", [128, NT, 128], F32)
        QTB = sb("QTB", [64, NT, 128], F32)
        KVB = sb("KVB", [64, NT, 128], F32)
        YTH = sb("YTH", [128, T], BF16)
        INQK = [sb("INQK%d" % p, [128, 2, 64], F32) for p in range(4)]
        ING = [sb("ING%d" % p, [128, 128], F32) for p in range(4)]
        VB = [sb("VB%d" % p, [128, 128], BF16) for p in range(4)]
        W = {}
        for nm in ("A", "B", "RK"):
            W[nm] = [sb("%s%d" % (nm, p), [128, 64], F32) for p in range(4)]
        for nm in ("KZF", "KZB"):
            W[nm] = [sb("%s%d" % (nm, p), [128, 64], BF16) for p in range(4)]
        for nm in ("RQb", "RKb"):
            W[nm] = [sb("%s%d" % (nm, p), [128, 64], F32) for p in range(4)]
        for nm in ("QT", "KT"):
            W[nm] = [sb("%s%d" % (nm, p), [64, 128], BF16) for p in range(4)]
        W["QXF"] = [sb("QXF%d" % p, [64, 128], F32) for p in range(4)]
        for nm in ("PM",):
            W[nm] = [sb("%s%d" % (nm, p), [128, 128], BF16) for p in range(4)]
        for nm in ("O", "GSG", "JK", "Y"):
            W[nm] = [sb("%s%d" % (nm, p), [128, 128], F32) for p in range(4)]
        SS = [sb("SS%d" % p, [128, 4], F32) for p in range(4)]
        NR = 8
        SF32 = [sb("SF32_%d" % p, [64, 128], F32) for p in range(NR)]
        BK1 = [ps("BK1_%d" % p, [128, 512], F32) for p in range(4)]
        BK2 = [ps("BK2_%d" % p, [128, 512], F32) for p in range(4)]
        DMA("sp", RMASK[:], ctx.c_rmask, (), ["RMASK"])
        DMA("sp", RPP[:], ctx.c_rpp, (), ["RPP"])
        DMA("sp", RXI[:], ctx.c_rxi, (), ["RXI"])
        DMA("sp", ROPE[:], ctx.c_rope, (), ["ROPE"])
        DMA("sp", IDB[:], ctx.c_ident, (), ["IDB"])
        DMA("sp", RTN[:], ctx.rtn_bc[:, l * 768:(l + 1) * 768], (), ["RTN"])
        NL = 4
        for hd in range(6):
            hs = slice(hd * 128, (hd + 1) * 128)
            xo = hd * 256

            def fwd_lane(lane, hd=hd, hs=hs, xo=xo):
                p = lane
                P = lambda nm: W[nm][p][:]
                K_ = lambda nm: "%s%d" % (nm, p)
                for t in range(lane, NT, NL):
                    cur, nxt = t % NR, (t + 1) % NR
                    t0 = t * 128
                    src = ctx.hrows(t0)[:, RT_Q:RT_Q + 768].rearrange("t (s c) -> t s c", s=2)[:, :, hd * 64:(hd + 1) * 64]
                    DMA("sp", INQK[p][:], src, (), [K_("INQK")])
                    DMA("pool", VB[p][:], ctx.hrows(t0)[:, RT_V + hd * 128:RT_V + (hd + 1) * 128], (), [K_("VB")])
                    CC = ROPE[:, t, 0:64]
                    for si, dst in ((0, P("RQb")), (1, P("RK"))):
                        x = INQK[p][:, si, :]
                        TT("dve", P("A"), x, CC, ALU.mult, [K_("INQK"), "ROPE"], [K_("A")])
                        TT("dve", W["B"][p][:, 0:32], INQK[p][:, si, 32:64], ROPE[:, t, 64:96], ALU.mult, [K_("INQK"), "ROPE"], [K_("B")])
                        TT("dve", W["B"][p][:, 32:64], INQK[p][:, si, 0:32], ROPE[:, t, 96:128], ALU.mult, [K_("INQK"), "ROPE"], [K_("B")])
                        TT("dve", dst, P("A"), P("B"), ALU.add, [K_("A"), K_("B")], [K_("RQb") if si == 0 else K_("RK")])
                    ACT(P("KZF"), P("RK"), AF.Copy, [K_("RK"), "RPP"], [K_("KZF")], scale=RPP[:, hd, 0:1])
                    ACT(P("KZB"), P("RK"), AF.Copy, [K_("RK"), "RPP"], [K_("KZB")], scale=RPP[:, hd, 1:2])
                    yield
                    pt = BK2[p]
                    pkv = BK2[p][0:64, 256:512].rearrange("p (a b) -> p a b", a=2)
                    TR(pt[0:64, 0:128], P("RQb"), IDB[:], [K_("RQb"), "IDB"], [K_("PT")])
                    TR(pt[0:64, 128:256], P("RK"), IDB[:], [K_("RK"), "IDB"], [K_("PT")])
                    MM(pkv[:, 0, :], P("KZF"), VB[p][:], True, True, [K_("KZF"), K_("VB")], [K_("PT")])
                    MM(pkv[:, 1, :], P("KZB"), VB[p][:], True, True, [K_("KZB"), K_("VB")], [K_("PT")])
                    yield
                    CP("act", P("QT"), pt[0:64, 0:128], [K_("PT")], [K_("QT")])
                    CP("act", P("KT"), pt[0:64, 128:256], [K_("PT")], [K_("KT")])
                    TT("dve", P("QXF"), pt[0:64, 0:128], RXI[:, xo:xo + 128], ALU.mult, [K_("PT"), "RXI"], [K_("QXF")])
                    TT("dve", QTB[:, t, :], pt[0:64, 0:128], RXI[:, xo + 128:xo + 256], ALU.mult, [K_("PT"), "RXI"], ["QTB%d" % t])
                    STT("dve", SF32[nxt][:], SF32[cur][:], float(rcd[hd, 0]), pkv[:, 0, :], ALU.mult, ALU.add,
                        ["SF32_%d" % cur, K_("PT")], ["SF32_%d" % nxt])
                    CP("act", KVB[:, t, :], pkv[:, 1, :], [K_("PT")], ["KVB%d" % t])
                    yield
                    pa = BK1[p]
                    MM(pa[:, 0:128], P("KT"), P("QT"), True, True, [K_("KT"), K_("QT")], [K_("PA")])
                    yield
                    TT("dve", P("PM"), pa[:, 0:128], RMASK[:, hd, :], ALU.mult, [K_("PA"), "RMASK"], [K_("PM")])
                    yield
                    po = BK1[p]
                    MM(po[:, 128:256], P("PM"), VB[p][:], True, False, [K_("PM"), K_("VB")], [K_("PA")])
                    MM(po[:, 128:256], P("QXF"), SF32[cur][:], False, True, [K_("QXF"), "SF32_%d" % cur], [K_("PA")])
                    yield
                    CP("act", OF[:, t, :], po[:, 128:256], [K_("PA")], ["OF%d" % t])

            MEMSET("dve", SF32[0][:], 0.0, ["SF32_0"])
            run_streams([fwd_lane(i) for i in range(NL)])

            def bwd_lane(lane, hd=hd, hs=hs):
                p = lane
                P = lambda nm: W[nm][p][:]
                K_ = lambda nm: "%s%d" % (nm, p)
                for idx in range(lane, NT, NL):
                    t = NT - 1 - idx
                    cur, nxt = idx % NR, (idx + 1) % NR
                    t0 = t * 128
                    DMA("sp", ING[p][:], ctx.hrows(t0)[:, RT_G + hd * 128:RT_G + (hd + 1) * 128], (), [K_("ING")])
                    STT("dve", SF32[nxt][:], SF32[cur][:], float(rcd[hd, 1]), KVB[:, t, :], ALU.mult, ALU.add,
                        ["SF32_%d" % cur, "KVB%d" % t], ["SF32_%d" % nxt])
                    po = BK1[p]
                    MM(po[:, 128:256], QTB[:, t, :], SF32[cur][:], True, True, ["QTB%d" % t, "SF32_%d" % cur], [K_("PA")])
                    ACT(P("GSG"), ING[p][:], AF.Silu, [K_("ING")], [K_("GSG")])
                    TT("pool", P("GSG"), P("GSG"), RTN[:, hs], ALU.mult, [K_("GSG"), "RTN"], [K_("GSG")])
                    yield
                    TT("dve", P("O"), OF[:, t, :], po[:, 128:256], ALU.add, ["OF%d" % t, K_("PA")], [K_("O")])
                    ACT(P("JK"), P("O"), AF.Identity, [K_("O")], [K_("JK"), K_("SS")], accum=SS[p][:, 0:1])
                    yield
                    TS("dve", SS[p][:, 1:2], SS[p][:, 0:1], -1.0 / 128, None, ALU.mult, None, [K_("SS")], [K_("SS")])
                    ACT(P("JK"), P("O"), AF.Square, [K_("O"), K_("SS")], [K_("JK"), K_("SS")], bias=SS[p][:, 1:2], accum=SS[p][:, 2:3])
                    ACT(SS[p][:, 3:4], SS[p][:, 2:3], AF.Sqrt, [K_("SS")], [K_("SS")], bias=EPS, scale=1.0 / 128)
                    yield
                    RECIP(SS[p][:, 3:4], SS[p][:, 3:4], [K_("SS")], [K_("SS")])
                    TS("dve", P("O"), P("O"), SS[p][:, 1:2], SS[p][:, 3:4], ALU.add, ALU.mult, [K_("O"), K_("SS")], [K_("O")])
                    TT("dve", P("Y"), P("O"), P("GSG"), ALU.mult, [K_("O"), K_("GSG")], [K_("Y")])
                    pt = BK1[p]
                    TR(pt[:, 256:384], P("Y"), IDB[:], [K_("Y"), "IDB"], [K_("PA")])
                    yield
                    CP("act", YTH[:, t0:t0 + 128], pt[:, 256:384], [K_("PA")], ["YTH"])

            MEMSET("dve", SF32[0][:], 0.0, ["SF32_0"])
            run_streams([bwd_lane(i) for i in range(NL)])
            DMA("sp", yT[768 + hd * 128:768 + (hd + 1) * 128, :], YTH[:], ["YTH"], ())
    S.barrier()


def mixer_dil(ctx, l):
    nc, S, T, NT = ctx.nc, ctx.S, ctx.T, ctx.NT
    MM, TR, ACT, TT, TS, STT, CP, DMA, MEMSET, RECIP = (ctx.MM, ctx.TR, ctx.ACT, ctx.TT, ctx.TS, ctx.STT, ctx.CP,
                                                        ctx.DMA, ctx.MEMSET, ctx.RECIP)
    yT, og = ctx.yT, ctx.og
    with contextlib.ExitStack() as st:
        def sb(name, shape, dt):
            return st.enter_context(nc.sbuf_tensor("dl%d_%s" % (l, name), shape, dt))

        def ps(name, shape, dt):
            return st.enter_context(nc.psum_tensor("dlp%d_%s" % (l, name), shape, dt))

        KTW = T + 128 * 16
        NVT = T // 128 + 16
        DB = sb("DB", [128, 3 * 4 * 4 * 256], F32)
        IDB = sb("IDB", [128, 128], F32)
        QG = sb("QG", [128, 128], F32)
        KG = sb("KG", [128, 128], F32)
        QTA = sb("QTA", [128, T], BF16)
        KTP = sb("KTP", [128, KTW], BF16)
        VBA = sb("VBA", [128, NVT, 128], BF16)
        NL = 4
        INQK = [sb("INQK%d" % p, [128, 2, 128], F32) for p in range(NL)]
        QN = [sb("QN%d" % p, [128, 2, 128], F32) for p in range(NL)]
        JK = [sb("JK%d" % p, [128, 128], F32) for p in range(NL)]
        SS = [sb("SS%d" % p, [128, 8], F32) for p in range(NL)]
        SB_ = [sb("SB%d" % p, [128, 256], F32) for p in range(NL)]
        PB_ = [sb("PB%d" % p, [128, 256], F32) for p in range(NL)]
        PTS = [sb("PTS%d" % p, [128, 2, 128], BF16) for p in range(NL)]
        OS = [sb("OS%d" % p, [128, 256], F32) for p in range(NL)]
        PS_ = [ps("PS%d" % p, [128, 512], F32) for p in range(NL)]
        PT = [ps("PT%d" % p, [128, 512], F32) for p in range(NL)]
        DMA("sp", DB[:], ctx.dbias, (), ["DB"])
        DMA("sp", IDB[:], ctx.c_ident, (), ["IDB"])
        DMA("sp", QG[:], ctx.qn_bc[:, l * 128:(l + 1) * 128], (), ["QG"])
        DMA("sp", KG[:], ctx.kn_bc[:, l * 128:(l + 1) * 128], (), ["KG"])
        TS("dve", QG[:], QG[:], 128.0 ** -0.5, None, ALU.mult, None, ["QG"], ["QG"])
        for p in range(NL):
            MEMSET("pool", OS[p][:], 0.0, ["OS%d" % p])
        for g, dil in enumerate((1, 4, 16)):
            L = T // dil
            nblk = L // 128
            nvt = nblk + 1
            lw = 128 // dil
            for s in range(4):
                base = DL0 + g * 1536
                MEMSET("pool", KTP[:], 0.0, ["KTP"])
                MEMSET("pool", VBA[:], 0.0, ["VBAz"] + ["VBA_t%d" % i for i in range(NVT)])
                vcol = base + 1024 + s * 128
                for r in range(dil):
                    for m in range(nvt):
                        lo = max(128 * m - 64, 0)
                        hi = min(128 * m + 64, L)
                        p0 = lo - (128 * m - 64)
                        n = hi - lo
                        i0 = 0
                        while i0 < n:
                            row0 = (lo + i0) * dil + r
                            part = row0 // ctx.HR
                            lim = (part + 1) * ctx.HR
                            n1 = min(n - i0, (lim - row0 + dil - 1) // dil)
                            lr = row0 - part * ctx.HR
                            rows = ctx.hparts[part][lr:lr + (n1 - 1) * dil + 1:dil, vcol:vcol + 128]
                            DMA("pool", VBA[p0 + i0:p0 + i0 + n1, r * nvt + m, :], rows, ["VBAz"], ["VBA_t%d" % (r * nvt + m)])
                            i0 += n1
                def prep_lane(lane, g=g, dil=dil, L=L, lw=lw, s=s, base=base):
                    p = lane
                    K_ = lambda nm: "%s%d" % (nm, p)
                    for t in range(lane, NT, NL):
                        t0 = t * 128
                        src = ctx.hrows(t0)[:, base:base + 1024].rearrange("t (s c) -> t s c", s=2)[:, :, s * 128:(s + 1) * 128]
                        DMA("sp", INQK[p][:], src, (), [K_("INQK")])
                        ACT(JK[p][:], INQK[p][:, 0, :], AF.Square, [K_("INQK")], [K_("JK"), K_("SSa")], accum=SS[p][:, 0:1])
                        ACT(JK[p][:], INQK[p][:, 1, :], AF.Square, [K_("INQK")], [K_("JK"), K_("SSb")], accum=SS[p][:, 1:2])
                        ACT(SS[p][:, 2:4], SS[p][:, 0:2], AF.Sqrt, [K_("SSa"), K_("SSb")], [K_("SSr")], bias=EPS, scale=1.0 / 128)
                        yield
                        RECIP(SS[p][:, 2:4], SS[p][:, 2:4], [K_("SSr")], [K_("SSr")])
                        STT("dve", QN[p][:, 0, :], INQK[p][:, 0, :], SS[p][:, 2:3], QG[:], ALU.mult, ALU.mult,
                            [K_("INQK"), K_("SSr"), "QG"], [K_("QN0")])
                        STT("dve", QN[p][:, 1, :], INQK[p][:, 1, :], SS[p][:, 3:4], KG[:], ALU.mult, ALU.mult,
                            [K_("INQK"), K_("SSr"), "KG"], [K_("QN1")])
                        pt = PT[p]
                        TR(pt[:, 0:128], QN[p][:, 0, :], IDB[:], [K_("QN0"), "IDB"], [K_("PT")])
                        TR(pt[:, 128:256], QN[p][:, 1, :], IDB[:], [K_("QN1"), "IDB"], [K_("PT")])
                        yield
                        l0 = t0 // dil
                        if dil == 1:
                            CP("act", QTA[:, t0:t0 + 128], pt[:, 0:128], [K_("PT")], ["QTA"])
                            CP("dve", KTP[:, 64 + t0:64 + t0 + 128], pt[:, 128:256], [K_("PT")], ["KTP"])
                        else:
                            CP("act", QTA[:, :].rearrange("d (r x) -> d r x", r=dil)[:, :, l0:l0 + lw],
                               pt[:, 0:128].rearrange("d (x r) -> d r x", r=dil), [K_("PT")], ["QTA"])
                            CP("dve", KTP[:, 0:dil * (L + 128)].rearrange("d (r x) -> d r x", r=dil)[:, :, 64 + l0:64 + l0 + lw],
                               pt[:, 128:256].rearrange("d (x r) -> d r x", r=dil), [K_("PT")], ["KTP"])

                run_streams([prep_lane(i) for i in range(NL)])

                blocks = [(r, b) for r in range(dil) for b in range(nblk)]

                def blk_lane(lane, g=g, dil=dil, L=L, nblk=nblk, nvt=nvt, s=s):
                    p = lane
                    K_ = lambda nm: "%s%d" % (nm, p)
                    for (r, b) in blocks[lane::NL]:
                        var = (1 if b == 0 else 0) + (2 if b == nblk - 1 else 0)
                        bo = ((g * 4 + s) * 4 + var) * 256
                        q0 = r * L + 128 * b
                        k0 = r * (L + 128) + 128 * b
                        pss = PS_[p]
                        MM(pss[:, 0:256], QTA[:, q0:q0 + 128], KTP[:, k0:k0 + 256], True, True, ["QTA", "KTP"], [K_("PS")])
                        yield
                        TT("dve", SB_[p][:], pss[:, 0:256], DB[:, bo:bo + 256], ALU.add, [K_("PS"), "DB"], [K_("SB")])
                        S.dve(_red_max(SS[p][:, 4:5], SB_[p][:]), [K_("SB")], [K_("SSm")])
                        TS("dve", SS[p][:, 5:6], SS[p][:, 4:5], -1.0, None, ALU.mult, None, [K_("SSm")], [K_("SSn")])
                        yield
                        ACT(PB_[p][:], SB_[p][:], AF.Exp, [K_("SB"), K_("SSn")], [K_("PB"), K_("SSd")],
                            bias=SS[p][:, 5:6], accum=SS[p][:, 6:7])
                        pt = PT[p]
                        TR(pt[:, 0:128], PB_[p][:, 0:128], IDB[:], [K_("PB"), "IDB"], [K_("PT")])
                        TR(pt[:, 128:256], PB_[p][:, 128:256], IDB[:], [K_("PB"), "IDB"], [K_("PT")])
                        yield
                        CP("act", PTS[p][:, 0, :], pt[:, 0:128], [K_("PT")], [K_("PTS0")])
                        CP("dve", PTS[p][:, 1, :], pt[:, 128:256], [K_("PT")], [K_("PTS1")])
                        RECIP(SS[p][:, 7:8], SS[p][:, 6:7], [K_("SSd")], [K_("SSi")])
                        MM(pss[:, 256:384], PTS[p][:, 0, :], VBA[:, r * nvt + b, :], True, False, [K_("PTS0"), "VBA_t%d" % (r * nvt + b)], [K_("PS")])
                        MM(pss[:, 256:384], PTS[p][:, 1, :], VBA[:, r * nvt + b + 1, :], False, True, [K_("PTS1"), "VBA_t%d" % (r * nvt + b + 1)], [K_("PS")])
                        yield
                        TS("dve", OS[p][:, 0:128], pss[:, 256:384], SS[p][:, 7:8], None, ALU.mult, None, [K_("PS"), K_("SSi")], ["OS%d" % p])
                        ACT(OS[p][:, 128:129], SS[p][:, 6:7], AF.Ln, [K_("SSd")], ["OS%d" % p])
                        TT("dve", OS[p][:, 128:129], OS[p][:, 128:129], SS[p][:, 4:5], ALU.add, ["OS%d" % p, K_("SSm")], ["OS%d" % p])
                        r0 = (128 * b) * dil + r
                        dst = og[g, s, r0:r0 + 127 * dil + 1:dil, :]
                        DMA("sp", dst, OS[p][:], ["OS%d" % p], ())

                run_streams([blk_lane(i) for i in range(NL)])
    S.barrier()
    with contextlib.ExitStack() as st:
        def sb(name, shape, dt):
            return st.enter_context(nc.sbuf_tensor("dc%d_%s" % (l, name), shape, dt))

        def ps(name, shape, dt):
            return st.enter_context(nc.psum_tensor("dcp%d_%s" % (l, name), shape, dt))

        IDB = sb("IDB", [128, 128], F32)
        YTS = sb("YTS", [128, T], BF16)
        IN3 = [sb("IN3_%d" % p, [128, 3, 256], F32) for p in range(2)]
        SS = [sb("SS%d" % p, [128, 12], F32) for p in range(2)]
        Y32 = [sb("Y32_%d" % p, [128, 128], F32) for p in range(2)]
        YB = [sb("YB%d" % p, [128, 128], F32) for p in range(2)]
        PT = [ps("PT%d" % p, [128, 512], F32) for p in range(2)]
        DMA("sp", IDB[:], ctx.c_ident, (), ["IDB"])
        cnt = dict(p=0)
        for s in range(4):
            for t in range(NT):
                p = cnt["p"] % 2
                cnt["p"] += 1
                K_ = lambda nm: "%s%d" % (nm, p)
                t0 = t * 128
                DMA(ctx.dq(), IN3[p][:], og[:, s, t0:t0 + 128, :].rearrange("g t c -> t g c"), (), [K_("IN3")])
                lse = IN3[p][:, :, 128]
                TT("dve", SS[p][:, 0:1], IN3[p][:, 0, 128:129], IN3[p][:, 1, 128:129], ALU.max, [K_("IN3")], [K_("SSm")])
                TT("dve", SS[p][:, 0:1], SS[p][:, 0:1], IN3[p][:, 2, 128:129], ALU.max, [K_("IN3"), K_("SSm")], [K_("SSm")])
                TS("dve", SS[p][:, 1:2], SS[p][:, 0:1], -1.0, None, ALU.mult, None, [K_("SSm")], [K_("SSn")])
                ACT(SS[p][:, 4:7], lse, AF.Exp, [K_("IN3"), K_("SSn")], [K_("SSe"), K_("SSd")],
                    bias=SS[p][:, 1:2], accum=SS[p][:, 2:3])
                RECIP(SS[p][:, 3:4], SS[p][:, 2:3], [K_("SSd")], [K_("SSi")])
                TS("dve", SS[p][:, 8:11], SS[p][:, 4:7], SS[p][:, 3:4], None, ALU.mult, None, [K_("SSe"), K_("SSi")], [K_("SSw")])
                TS("dve", Y32[p][:], IN3[p][:, 0, 0:128], SS[p][:, 8:9], None, ALU.mult, None, [K_("IN3"), K_("SSw")], [K_("Y32")])
                STT("dve", Y32[p][:], IN3[p][:, 1, 0:128], SS[p][:, 9:10], Y32[p][:], ALU.mult, ALU.add,
                    [K_("IN3"), K_("SSw"), K_("Y32")], [K_("Y32")])
                STT("dve", YB[p][:], IN3[p][:, 2, 0:128], SS[p][:, 10:11], Y32[p][:], ALU.mult, ALU.add,
                    [K_("IN3"), K_("SSw"), K_("Y32")], [K_("YB")])
                TR(PT[p][:, 0:128], YB[p][:], IDB[:], [K_("YB"), "IDB"], [K_("PT")])
                CP("act", YTS[:, t0:t0 + 128], PT[p][:, 0:128], [K_("PT")], ["YTS"])
            DMA("sp", yT[1536 + s * 128:1536 + (s + 1) * 128, :], YTS[:], ["YTS"], ())
    S.barrier()


def _red_max(out, in_):
    return lambda e: e.reduce_max(out=out, in_=in_, axis=AX.X)


def host_inputs(inp, b, T, DEPTH):
    f = lambda a: np.ascontiguousarray(np.asarray(a, dtype=np.float32))
    m = {}
    m["xT"] = f(np.asarray(inp["x"])[b, :T, :].T)
    for k in ("w_in", "w_out", "w_up", "w_down"):
        m[k] = f(np.asarray(inp[k])[:DEPTH])
    m["gmix"] = f(np.asarray(inp["norm_mix"])[:DEPTH].reshape(DEPTH, 16, 128).transpose(2, 0, 1).reshape(128, DEPTH * 16))
    m["gmlp"] = f(np.asarray(inp["norm_mlp"])[:DEPTH].reshape(DEPTH, 16, 128).transpose(2, 0, 1).reshape(128, DEPTH * 16))
    bc = lambda a: f(np.broadcast_to(np.asarray(a).reshape(1, -1), (128, np.asarray(a).size)))
    m["lbf_bc"] = bc(np.asarray(inp["hg_lb_fwd"]))
    m["lbb_bc"] = bc(np.asarray(inp["hg_lb_bwd"]))
    m["hgn_bc"] = bc(np.asarray(inp["hg_norm"])[:DEPTH])
    m["rtn_bc"] = bc(np.asarray(inp["ret_norm"])[:DEPTH])
    m["qn_bc"] = bc(np.asarray(inp["q_norm"])[:DEPTH])
    m["kn_bc"] = bc(np.asarray(inp["k_norm"])[:DEPTH])
    m["dbias"] = dil_bias_tables(np.asarray(inp["rel_bias"], dtype=np.float32))
    c = host_consts(T)
    for k in ("rope", "hgm", "ind", "ident", "rmask", "rpp", "rxi"):
        m[k] = f(c[k])
    return m


_CACHE = {}


def kernel(**inputs):
    T, DEPTH = 8192, 4
    if "nc" not in _CACHE:
        _CACHE["nc"] = build(T, DEPTH)[0]
    nc = _CACHE["nc"]
    maps = [host_inputs(inputs, c % 2, T, DEPTH) for c in range(2)]
    in_maps = [maps[c % 2] for c in range(8)]
    res = run_bass_kernel_spmd(nc, in_maps, core_ids=list(range(8)))
    out = np.stack([np.ascontiguousarray(res.results[b]["outT"].T) for b in range(2)], axis=0)
    return out.astype(np.float32)
```
